# Optimizing a Trainium2 kernel written in Bass

```python
import math
import jax, jax.numpy as jnp
from jax import lax
import numpy as np

D_MODEL = 1024
BATCH = 16
SEQ = 2048
DEPTH = 4

CHUNK = 64
Q_BLOCK = 128
SSD_HEADS = 8
SSD_HEAD_DIM = 64
SSD_WIDTH = SSD_HEADS * SSD_HEAD_DIM
SSD_GROUPS = 2
SSD_STATE = 128
SSD_CONV = 4
SSD_CHUNK = CHUNK
DIFF_HEADS = 4
DIFF_HEAD_DIM = 64
DIFF_V_DIM = 2 * DIFF_HEAD_DIM
DIFF_WIDTH = DIFF_HEADS * DIFF_V_DIM
ROT_DIM = DIFF_HEAD_DIM // 4
ROPE_THETA = 500000.0
MIX_WIDTH = SSD_WIDTH + DIFF_WIDTH
CONV_CH = SSD_WIDTH + 2 * SSD_GROUPS * SSD_STATE
QK_WIDTH = DIFF_HEADS * 2 * DIFF_HEAD_DIM
IN_SPLITS = (SSD_WIDTH, SSD_WIDTH + CONV_CH, SSD_WIDTH + CONV_CH + SSD_HEADS,
             SSD_WIDTH + CONV_CH + SSD_HEADS + QK_WIDTH,
             SSD_WIDTH + CONV_CH + SSD_HEADS + 2 * QK_WIDTH)
IN_COLS = SSD_WIDTH + CONV_CH + SSD_HEADS + 2 * QK_WIDTH + DIFF_WIDTH
D_FF = 2816
FFN_RES_WEIGHT = 0.5
DEEPNORM_ALPHA = (2 * DEPTH) ** 0.25
DEEPNORM_BETA = (8 * DEPTH) ** -0.25
N_SUB = 3
LN_EPS = 1e-5

kernel_name = "hybrid_ssd_diffattn_macaron_deepnorm_adaln"


def _standardize(x):
    xf = x.astype(jnp.float32)
    mu = jnp.mean(xf, axis=-1, keepdims=True)
    var = jnp.mean(jnp.square(xf - mu), axis=-1, keepdims=True)
    return ((xf - mu) * lax.rsqrt(var + LN_EPS)).astype(x.dtype)


def layer_norm(x, g, b):
    return _standardize(x) * g + b


def rms_norm(x, w):
    xf = x.astype(jnp.float32)
    y = xf * lax.rsqrt(jnp.mean(jnp.square(xf), axis=-1, keepdims=True) + LN_EPS)
    return y.astype(x.dtype) * w


def modulate(x, shift, scale):
    return _standardize(x) * (1 + scale) + shift


def swiglu(u, w_in, w_out):
    g, up = jnp.split(u @ w_in, 2, axis=-1)
    return (jax.nn.silu(g) * up) @ w_out


def causal_depthwise_conv(x, w, b):
    out = lax.conv_general_dilated(
        x, w[:, None, :], window_strides=(1,), padding=[(SSD_CONV - 1, 0)],
        dimension_numbers=('NWC', 'WIO', 'NWC'), feature_group_count=x.shape[-1])
    return out + b


def ssd_scan(x, dt, A, Bm, Cm, d_skip):
    b, L = x.shape[:2]
    nc = L // SSD_CHUNK
    r = SSD_HEADS // SSD_GROUPS
    dt_x = dt.astype(x.dtype)
    xdt = (x * dt_x[..., None]).reshape(b, nc, SSD_CHUNK, SSD_GROUPS, r, SSD_HEAD_DIM)
    a = (dt * A).reshape(b, nc, SSD_CHUNK, SSD_GROUPS, r)
    a_cs = jnp.cumsum(a, axis=2)
    Bc = Bm.reshape(b, nc, SSD_CHUNK, SSD_GROUPS, SSD_STATE)
    Cc = Cm.reshape(b, nc, SSD_CHUNK, SSD_GROUPS, SSD_STATE)
    seg = a_cs[:, :, :, None] - a_cs[:, :, None, :]
    causal = jnp.tril(jnp.ones((SSD_CHUNK, SSD_CHUNK), dtype=bool))[:, :, None, None]
    decay = jnp.exp(jnp.where(causal, seg, -jnp.inf)).astype(x.dtype)
    y_diag = jnp.einsum('bclgn,bcsgn,bclsgr,bcsgrp->bclgrp', Cc, Bc, decay, xdt)
    decay_to_end = jnp.exp(a_cs[:, :, -1:] - a_cs).astype(x.dtype)
    states = jnp.einsum('bclgn,bclgr,bclgrp->bcgrpn', Bc, decay_to_end, xdt)
    chunk_decay = jnp.exp(a_cs[:, :, -1]).astype(x.dtype)

    def step(h, inp):
        st, dec = inp
        return h * dec[..., None, None] + st, h

    h0 = jnp.zeros((b, SSD_GROUPS, r, SSD_HEAD_DIM, SSD_STATE), states.dtype)
    _, prev = lax.scan(step, h0, (jnp.moveaxis(states, 1, 0), jnp.moveaxis(chunk_decay, 1, 0)))
    prev = jnp.moveaxis(prev, 0, 1)
    y_off = jnp.einsum('bclgn,bcgrpn,bclgr->bclgrp', Cc, prev, jnp.exp(a_cs).astype(x.dtype))
    y = (y_diag + y_off).reshape(b, L, SSD_HEADS, SSD_HEAD_DIM)
    return y + x * d_skip[:, None]


def partial_rope(t, pos):
    half = ROT_DIM // 2
    inv = ROPE_THETA ** (-jnp.arange(half, dtype=jnp.float32) * 2.0 / ROT_DIM)
    ang = pos.astype(jnp.float32)[..., None] * inv
    cos = jnp.cos(ang)[:, :, None, None, :].astype(t.dtype)
    sin = jnp.sin(ang)[:, :, None, None, :].astype(t.dtype)
    t1, t2, rest = t[..., :half], t[..., half:ROT_DIM], t[..., ROT_DIM:]
    return jnp.concatenate([t1 * cos - t2 * sin, t2 * cos + t1 * sin, rest], axis=-1)


def diff_attention(q, k, v, lam, lam_init, subln_w):
    b, L = q.shape[:2]
    nblk = L // Q_BLOCK
    scale = DIFF_HEAD_DIM ** -0.5
    kchunk = jnp.arange(L) // CHUNK
    qb = jnp.moveaxis(q.reshape(b, nblk, Q_BLOCK, DIFF_HEADS, 2, DIFF_HEAD_DIM), 1, 0)

    def block(args):
        qi, i = args
        s = jnp.einsum('bqhjd,bkhjd->bhjqk', qi, k).astype(jnp.float32) * scale
        qchunk = (i * Q_BLOCK + jnp.arange(Q_BLOCK)) // CHUNK
        mask = kchunk[None, :] <= qchunk[:, None]
        p = jax.nn.softmax(jnp.where(mask, s, -jnp.inf), axis=-1)
        attn = p[:, :, 0] - lam * p[:, :, 1]
        return jnp.einsum('bhqk,bkhe->bqhe', attn.astype(v.dtype), v)

    o = lax.map(block, (qb, jnp.arange(nblk)))
    o = jnp.moveaxis(o, 0, 1).reshape(b, L, DIFF_HEADS, DIFF_V_DIM)
    o = rms_norm(o, subln_w) * (1.0 - lam_init)
    return o.reshape(b, L, DIFF_WIDTH)


def hybrid_mixer(u, pos, w_in, conv_w, conv_b, dt_bias, a_log, d_skip, ssd_norm_w,
                 diff_lambda, subln_w, w_out, lam_init):
    b, L, _ = u.shape
    proj = u @ w_in
    z, xbc, dt_raw, q, k, v = jnp.split(proj, IN_SPLITS, axis=-1)
    xbc = jax.nn.silu(causal_depthwise_conv(xbc, conv_w, conv_b))
    xs, Bm, Cm = jnp.split(xbc, (SSD_WIDTH, SSD_WIDTH + SSD_GROUPS * SSD_STATE), axis=-1)
    dt = jax.nn.softplus((dt_raw + dt_bias).astype(jnp.float32))
    A = -jnp.exp(a_log.astype(jnp.float32))
    y = ssd_scan(xs.reshape(b, L, SSD_HEADS, SSD_HEAD_DIM), dt, A,
                 Bm.reshape(b, L, SSD_GROUPS, SSD_STATE), Cm.reshape(b, L, SSD_GROUPS, SSD_STATE), d_skip)
    y = y.reshape(b, L, SSD_WIDTH) * jax.nn.silu(z)
    y = rms_norm(y.reshape(b, L, SSD_GROUPS, SSD_WIDTH // SSD_GROUPS),
                 ssd_norm_w.reshape(SSD_GROUPS, SSD_WIDTH // SSD_GROUPS)).reshape(b, L, SSD_WIDTH)
    q = partial_rope(q.reshape(b, L, DIFF_HEADS, 2, DIFF_HEAD_DIM), pos)
    k = partial_rope(k.reshape(b, L, DIFF_HEADS, 2, DIFF_HEAD_DIM), pos)
    v = v.reshape(b, L, DIFF_HEADS, DIFF_V_DIM)
    lp = diff_lambda.astype(jnp.float32)
    lam = jnp.exp(jnp.sum(lp[0] * lp[1])) - jnp.exp(jnp.sum(lp[2] * lp[3])) + lam_init
    o = diff_attention(q, k, v, lam, lam_init, subln_w)
    return jnp.concatenate([y, o], axis=-1) @ w_out


def setup_inputs(seed: int = 0) -> dict:
    key = jax.random.key(seed)
    ks = jax.random.split(key, 20)
    f32 = jnp.float32
    x = jax.random.normal(ks[0], (BATCH, SEQ, D_MODEL), f32)
    c = jax.random.normal(ks[1], (BATCH, D_MODEL), f32)
    offset = jax.random.randint(ks[2], (BATCH, 1), 0, 64, dtype=jnp.int32) * CHUNK
    positions = (offset + jnp.arange(SEQ, dtype=jnp.int32)[None, :]).astype(jnp.int32)
    w_ada = jax.random.normal(ks[3], (DEPTH, D_MODEL, N_SUB * 3 * D_MODEL), f32) * (0.2 * D_MODEL ** -0.5)
    b_ada = jax.random.normal(ks[4], (DEPTH, N_SUB * 3 * D_MODEL), f32) * 0.01
    w_ffn_in = jax.random.normal(ks[5], (DEPTH, 2, D_MODEL, 2 * D_FF), f32) * D_MODEL ** -0.5
    w_ffn_out = jax.random.normal(ks[6], (DEPTH, 2, D_FF, D_MODEL), f32) * (D_FF ** -0.5 * DEEPNORM_BETA)
    w_in = jax.random.normal(ks[7], (DEPTH, D_MODEL, IN_COLS), f32) * D_MODEL ** -0.5
    conv_w = jax.random.normal(ks[8], (DEPTH, SSD_CONV, CONV_CH), f32) * SSD_CONV ** -0.5
    conv_b = jax.random.normal(ks[9], (DEPTH, CONV_CH), f32) * 0.01
    u_dt = jax.random.uniform(ks[10], (DEPTH, SSD_HEADS), f32)
    dt0 = jnp.exp(u_dt * (math.log(0.1) - math.log(0.001)) + math.log(0.001))
    dt0 = jnp.maximum(dt0, 1e-4)
    dt_bias = dt0 + jnp.log(-jnp.expm1(-dt0))
    a_log = jnp.log(jax.random.uniform(ks[11], (DEPTH, SSD_HEADS), f32, 1.0, 16.0))
    d_skip = 1.0 + 0.01 * jax.random.normal(ks[12], (DEPTH, SSD_HEADS), f32)
    ssd_norm_w = 1.0 + 0.01 * jax.random.normal(ks[13], (DEPTH, SSD_WIDTH), f32)
    diff_lambda = 0.1 * jax.random.normal(ks[14], (DEPTH, 4, DIFF_HEAD_DIM), f32)
    subln_w = 1.0 + 0.01 * jax.random.normal(ks[15], (DEPTH, DIFF_V_DIM), f32)
    w_out = jax.random.normal(ks[16], (DEPTH, MIX_WIDTH, D_MODEL), f32) * (MIX_WIDTH ** -0.5 * DEEPNORM_BETA)
    ln_g = 1.0 + 0.01 * jax.random.normal(ks[17], (DEPTH, N_SUB, D_MODEL), f32)
    ln_b = 0.01 * jax.random.normal(ks[18], (DEPTH, N_SUB, D_MODEL), f32)
    return {"x": x, "c": c, "positions": positions, "w_ada": w_ada, "b_ada": b_ada,
            "w_ffn_in": w_ffn_in, "w_ffn_out": w_ffn_out, "w_in": w_in, "conv_w": conv_w,
            "conv_b": conv_b, "dt_bias": dt_bias, "a_log": a_log, "d_skip": d_skip,
            "ssd_norm_w": ssd_norm_w, "diff_lambda": diff_lambda, "subln_w": subln_w,
            "w_out": w_out, "ln_g": ln_g, "ln_b": ln_b}


def reference(x, c, positions, w_ada, b_ada, w_ffn_in, w_ffn_out, w_in, conv_w, conv_b,
              dt_bias, a_log, d_skip, ssd_norm_w, diff_lambda, subln_w, w_out, ln_g, ln_b):
    b = x.shape[0]
    for layer in range(DEPTH):
        mod = (jax.nn.silu(c) @ w_ada[layer] + b_ada[layer]).reshape(b, N_SUB, 3, 1, D_MODEL)
        lam_init = 0.8 - 0.6 * math.exp(-0.3 * layer)
        u = modulate(x, mod[:, 0, 0], mod[:, 0, 1])
        f = swiglu(u, w_ffn_in[layer, 0], w_ffn_out[layer, 0])
        x = layer_norm(DEEPNORM_ALPHA * x + FFN_RES_WEIGHT * (1 + mod[:, 0, 2]) * f,
                       ln_g[layer, 0], ln_b[layer, 0])
        u = modulate(x, mod[:, 1, 0], mod[:, 1, 1])
        m = hybrid_mixer(u, positions, w_in[layer], conv_w[layer], conv_b[layer], dt_bias[layer],
                         a_log[layer], d_skip[layer], ssd_norm_w[layer], diff_lambda[layer],
                         subln_w[layer], w_out[layer], lam_init)
        x = layer_norm(DEEPNORM_ALPHA * x + (1 + mod[:, 1, 2]) * m, ln_g[layer, 1], ln_b[layer, 1])
        u = modulate(x, mod[:, 2, 0], mod[:, 2, 1])
        f = swiglu(u, w_ffn_in[layer, 1], w_ffn_out[layer, 1])
        x = layer_norm(DEEPNORM_ALPHA * x + FFN_RES_WEIGHT * (1 + mod[:, 2, 2]) * f,
                       ln_g[layer, 2], ln_b[layer, 2])
    return x
```

```python
import math
import types
import numpy as np
import concourse.bass as bass
import concourse.mybir as mybir
from concourse.bass_utils import run_bass_kernel_spmd

F32 = mybir.dt.float32
BF16 = mybir.dt.bfloat16
I32 = mybir.dt.int32
AF = mybir.ActivationFunctionType
ALU = mybir.AluOpType

D = 1024
T = 2048
NT = 16
TB = 512
NB = 4
DFF = 2816
NJ = 22
NJG = 11
DEPTH = 4
INC = 3080
ALPHA = (2 * DEPTH) ** 0.25
EPS = 1e-5
NEG = -30000.0
TWO_PI = 2.0 * math.pi


def _freeze(fn):
    if fn is None or fn.__closure__ is None:
        return fn
    cells = []
    for c in fn.__closure__:
        try:
            cells.append(types.CellType(c.cell_contents))
        except ValueError:
            cells.append(c)
    g = types.FunctionType(fn.__code__, fn.__globals__, fn.__name__, fn.__defaults__, tuple(cells))
    g.__kwdefaults__ = fn.__kwdefaults__
    return g


class Clock:
    __slots__ = ("sem", "count", "name")

    def __init__(self, sem, name):
        self.sem = sem
        self.count = 0
        self.name = name


class Tok:
    __slots__ = ("w", "r", "name")

    def __init__(self, name=""):
        self.w = None
        self.r = []
        self.name = name


class Prog:
    ENG = ("pe", "act", "dve", "pool", "sp")

    def __init__(self, nc):
        self.nc = nc
        self.clk = {k: Clock(nc.alloc_semaphore(name="c_" + k), k) for k in self.ENG}
        self.streams = {k: [] for k in self.ENG}
        self.known = {k: {} for k in self.ENG}
        self.ninstr = 0

    def new_clock(self, name):
        return Clock(self.nc.alloc_semaphore(name="d_" + name), name)

    def _waits(self, eng, reads, writes, same_ok):
        need = {}
        own = self.clk[eng]

        def req(cv):
            if cv is None:
                return
            c, v = cv
            if same_ok and c is own:
                return
            if need.get(c, 0) < v:
                need[c] = v

        for t in reads:
            req(t.w)
        for t in writes:
            req(t.w)
            for cv in t.r:
                req(cv)
        kn = self.known[eng]
        out = []
        for c, v in need.items():
            if kn.get(c, 0) >= v:
                continue
            kn[c] = v
            out.append((c.sem, v))
        return out

    def _mark(self, me, reads, writes):
        for t in reads:
            t.r.append(me)
            if len(t.r) > 64:
                best = {}
                for c, v in t.r:
                    if best.get(c, 0) < v:
                        best[c] = v
                t.r = list(best.items())
        for t in writes:
            t.w = me
            t.r = []

    def op(self, eng, fn, reads=(), writes=(), same_ok=False):
        waits = self._waits(eng, reads, writes, same_ok)
        c = self.clk[eng]
        c.count += 1
        me = (c, c.count)
        self.streams[eng].append((waits, _freeze(fn), c.sem, 1))
        self._mark(me, reads, writes)
        self.ninstr += 1
        return me

    def group(self, eng, fns, reads=(), writes=(), same_ok=False):
        waits = self._waits(eng, reads, writes, same_ok)
        c = self.clk[eng]
        c.count += 1
        me = (c, c.count)
        n = len(fns)
        for i, fn in enumerate(fns):
            self.streams[eng].append((waits if i == 0 else [], _freeze(fn), c.sem if i == n - 1 else None, 1))
        self._mark(me, reads, writes)
        self.ninstr += n
        return me

    def dma(self, q, dclk, pairs, reads=(), writes=(), **kw):
        waits = self._waits(q, reads, writes, False)
        for i, (o, s) in enumerate(pairs):
            dclk.count += 16
            self.streams[q].append((waits if i == 0 else [],
                                    (lambda e, o=o, s=s, k=kw: e.dma_start(out=o, in_=s, **k)),
                                    dclk.sem, 16))
        me = (dclk, dclk.count)
        self._mark(me, reads, writes)
        self.ninstr += len(pairs)
        return me

    def dma_throttled(self, q, clks, pairs, writes=()):
        n = len(clks)
        for i, (o, s) in enumerate(pairs):
            c = clks[i % n]
            w = [(c.sem, c.count)] if c.count > 0 else []
            c.count += 16
            self.streams[q].append((w, (lambda e, o=o, s=s: e.dma_start(out=o, in_=s)), c.sem, 16))
        for t, c in zip(writes, clks):
            t.w = (c, c.count)
            t.r = []
        self.ninstr += len(pairs)

    def barrier(self):
        self.last_barrier = [(self.clk[o].sem, self.clk[o].count) for o in ("pe", "act", "dve") if self.clk[o].count > 0]
        for e in ("pe", "act", "dve", "pool"):
            waits = []
            kn = self.known[e]
            for o in ("pe", "act", "dve", "pool"):
                if o == e:
                    continue
                c = self.clk[o]
                if c.count > kn.get(c, 0):
                    kn[c] = c.count
                    waits.append((c.sem, c.count))
            if waits:
                self.streams[e].append((waits, None, None, 0))

    def wait_barrier(self, q):
        if getattr(self, "last_barrier", None):
            self.streams[q].append((list(self.last_barrier), None, None, 0))

    def finish(self, toks):
        waits = self._waits("sp", [], toks, False)
        self.streams["sp"].append((waits, None, None, 0))

    def emit(self):
        nc = self.nc
        with nc.Block() as block:
            def run(name):
                def body(eng):
                    for waits, fn, sem, inc in self.streams[name]:
                        for s, v in waits:
                            eng.wait_ge(s, v)
                        if fn is None:
                            continue
                        ins = fn(eng)
                        if sem is not None:
                            ins.then_inc(sem, inc)
                return body
            block.tensor(run("pe"))
            block.scalar(run("act"))
            block.vector(run("dve"))
            block.gpsimd(run("pool"))
            block.sync(run("sp"))


class Slot:
    __slots__ = ("ap", "tok", "clk", "clk_sw")

    def __init__(self, ap, tok, clk=None, clk_sw=None):
        self.ap = ap
        self.tok = tok
        self.clk = clk
        self.clk_sw = clk_sw


NW = 2
ARENA = 22940
ATT_LA = 2


def build(nlayers=DEPTH, nseq=2, stop=None, dbg=None):
    nc = bass.Bass("TRN2", target_bir_lowering=False)
    P = Prog(nc)

    def din(name, shape, dt=F32):
        return nc.dram_tensor(name, list(shape), dt, kind="ExternalInput").ap()

    x_d = din("x", [2, T, D])
    cT_d = din("cT", [128, 8, 2])
    pos_d = din("pos", [128, 2, NT], I32)
    wada_d = din("w_ada", [DEPTH, D, 9 * D])
    badaT_d = din("b_adaT", [128, DEPTH, 72])
    wfi_d = din("w_ffn_in", [DEPTH, 2, D, 2 * DFF])
    wfo_d = din("w_ffn_out", [DEPTH, 2, DFF, D])
    win_d = din("w_in", [DEPTH, D, INC])
    wout_d = din("w_out", [DEPTH, D, D])
    convw_d = din("conv_wT", [128, DEPTH, 4, 8])
    convb_d = din("conv_bT", [128, DEPTH, 8])
    dtb_d = din("dtb", [128, DEPTH, 8])
    alog_d = din("alog", [128, DEPTH, 8])
    dsk_d = din("dsk", [128, DEPTH, 8])
    ssdnw_d = din("ssdnwT", [128, DEPTH, 4])
    subln_d = din("sublnT", [128, DEPTH])
    dlam_d = din("dlam", [128, DEPTH, 4, 64])
    lng_d = din("ln_g_rep", [128, DEPTH * 3 * D])
    lnb_d = din("ln_b_rep", [128, DEPTH * 3 * D])
    identf_d = din("identf", [128, 128])
    U_d = din("Umat", [128, 128])
    negm_d = din("negmask4", [128, 512])
    invf_d = din("invf", [128, 8])
    out_d = nc.dram_tensor("out", [2, T, D], F32, kind="ExternalOutput").ap()
    dbg_d = nc.dram_tensor("dbg", [128, 8192], F32, kind="ExternalOutput").ap() if dbg else None

    class StopBuild(Exception):
        pass

    def DBG(name, ap, toks):
        if dbg != name:
            return
        n = int(np.prod(ap.shape[1:]))
        src = ap
        if len(ap.shape) == 3:
            src = ap.rearrange("p a b -> p (a b)") if False else ap
        ck = P.new_clock("dbg")
        tk = Tok("dbg")
        dst = dbg_d[:, 0:n]
        if len(ap.shape) == 3:
            dst = dst.rearrange("p (a b) -> p a b", a=ap.shape[1])
        elif len(ap.shape) == 4:
            dst = dst.rearrange("p (a b c) -> p a b c", a=ap.shape[1], b=ap.shape[2])
        P.dma("pool", ck, [(dst, src)], reads=list(toks), writes=[tk])
        P.finish([tk])
        raise StopBuild()

    def dscr(name, shape):
        return nc.dram_tensor(name, list(shape), BF16, kind="Internal").ap()

    wbfi = dscr("wbfi", [DEPTH, 2, NJG, 128, 8, 2, 256])
    wbfo = dscr("wbfo", [DEPTH, 2, 2, 3, 128, 8, 512])
    wbmi = dscr("wbmi", [DEPTH, 6, 128, 8, 512])
    wbmd = dscr("wbmd", [DEPTH, 128, 8, 8])
    wbmo = dscr("wbmo", [DEPTH, 2, 128, 8, 512])

    def sb(name, shape, dt=F32):
        return nc.alloc_sbuf_tensor("s_" + name, list(shape), dt)

    X = sb("X", [128, NT, D])
    Xtok = [Tok(f"X{t}") for t in range(NT)]
    identf = sb("identf", [128, 128]); identb = sb("identb", [128, 128], BF16)
    Umat = sb("Umat", [128, 128]); negm = sb("negm", [128, 512]); onesf = sb("onesf", [128, 128])
    invf = sb("invf", [128, 8])
    cT = sb("cT", [128, 8, 2]); scT = sb("scT", [128, 8, 2], BF16)
    posi = sb("posi", [128, 2, NT], I32)
    badaT = sb("badaT", [128, DEPTH, 72])
    modT = sb("modT", [128, DEPTH, 72, 2])
    convw = sb("convw", [128, DEPTH, 4, 8]); convb = sb("convb", [128, DEPTH, 8])
    dtb = sb("dtb", [128, DEPTH, 8]); alog = sb("alog", [128, DEPTH, 8]); dsk = sb("dsk", [128, DEPTH, 8])
    Aneg = sb("Aneg", [128, DEPTH, 8])
    ssdnw = sb("ssdnw", [128, DEPTH, 4]); subln = sb("subln", [128, DEPTH]); sublnS = sb("sublnS", [128, DEPTH])
    lamt = sb("lamt", [128, DEPTH, 2]); nlam = sb("nlam", [128, DEPTH])
    cosT = sb("cosT", [128, NT, 8]); sinT = sb("sinT", [128, NT, 8])
    sc1 = sb("sc1", [128, 8]); gcol = sb("gcol", [128, 8]); epsT = sb("epsT", [128, 1])
    gate_b = sb("gate_b", [128, D]); lng_b = sb("lng_b", [128, D]); lnb_b = sb("lnb_b", [128, D])
    st6 = sb("st6", [128, 4, 2, 6]); mv = sb("mv", [128, 4, 2]); rstd = sb("rstd", [128, 4]); nmr = sb("nmr", [128, 4])
    est6 = sb("est6", [128, 2, 6]); emv = sb("emv", [128, 2]); erstd = sb("erstd", [128, 1]); enmr = sb("enmr", [128, 1])
    Wring = [sb(f"Wring{i}", [128, 8, 512], BF16) for i in range(NW)]
    uT0 = sb("uT0", [128, 8, TB], BF16)
    tmp5 = [sb(f"tmp5_{i}", [128, 512]) for i in range(3)]
    dg = [sb(f"dg{i}", [128, 128]) for i in range(2)]

    ARENA_F32 = ARENA
    arena = sb("arena", [128, ARENA_F32])

    class Carver:
        def __init__(self):
            self.off = 0

        def take(self, shape, dt=F32):
            n = int(np.prod(shape[1:]))
            nf = n if dt == F32 else (n + 1) // 2
            a = arena[:, self.off:self.off + nf]
            self.off += nf
            assert self.off <= ARENA_F32, self.off
            if dt != F32:
                a = a.bitcast(dt)[:, 0:n]
            if len(shape) == 3:
                a = a.rearrange("p (a b) -> p a b", a=shape[1])
            elif len(shape) == 4:
                a = a.rearrange("p (a b c) -> p a b c", a=shape[1], b=shape[2])
            return a

    cf = Carver()
    actT = cf.take([128, NJ, TB], BF16)
    uT1 = cf.take([128, 8, TB], BF16)
    xhat_f = [cf.take([128, D]) for _ in range(2)]
    dlam = cf.take([128, DEPTH, 4, 64])
    WF_extra = [cf.take([128, 8, 512], BF16) for _ in range(2)]
    uTb = [uT0[:], uT1]
    cm = Carver()
    KT = cm.take([128, 4, T], BF16)
    Vc = cm.take([128, NT, 4, 130], BF16)
    QT2 = [cm.take([128, 4, 128], BF16) for _ in range(2)]
    stag = [cm.take([128, 516])] * 2
    hist = cm.take([128, 8, 4])
    BCT = cm.take([128, 4, TB], BF16)
    Btm = cm.take([128, 4, 256], BF16)
    xs_tm = cm.take([128, 4, 512])
    xdt = cm.take([128, 4, 512], BF16)
    xdt2 = cm.take([128, 512], BF16)
    aU2 = [cm.take([128, 4, 128]) for _ in range(2)]
    Eb2 = [cm.take([128, 4, 128], BF16) for _ in range(2)]
    Mb = cm.take([128, 8, 128], BF16)
    PTr = [cm.take([128, 2, 4, 128], BF16) for _ in range(3)]
    yoT = cm.take([128, 8, TB], BF16)
    prev = cm.take([128, 512]); prevb = cm.take([128, 512], BF16)
    yo2 = cm.take([128, 1024])
    ytmp = yo2[:, 0:512]; otmp = yo2[:, 512:1024]; xhat_m = yo2
    qk_tm = [ytmp, cm.take([128, 512])]
    zs = cm.take([128, 512])
    ropet = zs[:, 0:256].rearrange("p (a b c) -> p a b c", a=8, b=4)
    dts = cm.take([128, 4, 8]); a_t = cm.take([128, 4, 8]); acs = cm.take([128, 4, 8]); nacs = cm.take([128, 4, 8])
    ea = cm.take([128, 4, 8]); cd = cm.take([128, 4, 8]); dte = cm.take([128, 4, 8]); dtd = cm.take([128, 4, 8])
    ss = cm.take([128, 8]); rr = cm.take([128, 8])
    print("arena floats: ffn", cf.off, "mixer", cm.off, "sbuf left", nc.sbuf_bytes_remaining)

    banks = [nc.alloc_psum_tensor(f"bank{i}", [128, 512], F32) for i in range(8)]
    btok = [Tok(f"bank{i}") for i in range(8)]

    def tok(n=""):
        return Tok(n)

    t_const = tok("const")
    t_modl = [tok(f"mod{l}") for l in range(DEPTH)]
    t_rows = {"gate": tok(), "lng": tok(), "lnb": tok()}
    t_sc = tok("sc1gcol")
    t_stat = tok("stat")
    t_estat = tok("estat")
    t_xhat_f = [tok("xhat0"), tok("xhat1")]
    t_tmp5 = [tok() for _ in range(3)]
    t_dg = [tok(), tok()]
    t_uT = [tok(), tok()]
    t_act = tok("actT")
    Wslots = [Slot(Wring[i], tok(f"W{i}"), P.new_clock(f"W{i}"), P.new_clock(f"Ws{i}")) for i in range(NW)]
    wctr = [0]

    Wslots_f = Wslots + [Slot(WF_extra[i], tok(f"WF{i}"), P.new_clock(f"WF{i}"), P.new_clock(f"WFs{i}")) for i in range(2)]
    wfctr = [0]

    arena_gate = {2: None, 3: None}

    def next_w(ffn=False):
        if ffn:
            idx = wfctr[0] % len(Wslots_f)
            s = Wslots_f[idx]
            wfctr[0] += 1
            if idx >= 2:
                lb = getattr(P, "last_barrier", None)
                if lb is not None and arena_gate[idx] is not lb:
                    arena_gate[idx] = lb
                    P.wait_barrier("sp")
            return s
        s = Wslots[wctr[0] % NW]
        wctr[0] += 1
        return s

    tm = {n: tok(n) for n in ("KT", "V", "QT", "hist", "BCT", "Btm", "xs", "xdt", "xdt2", "aU0", "aU1", "E0", "E1", "M", "yoT",
                               "prev", "prevb", "ytmp", "otmp", "zs", "dt", "ss", "stag0", "stag1",
                               "qk1", "pt0", "pt1", "pt2")}
    tm["qk0"] = tm["ytmp"]
    tm["QT0"] = tok("QT0"); tm["QT1"] = tok("QT1")
    tm["stag1"] = tm["stag0"]
    tm["rope"] = tm["zs"]

    c_const = P.new_clock("const")
    c_rows = P.new_clock("rows")
    c_x = [P.new_clock(f"x{b}") for b in range(NB)]
    c_out = P.new_clock("out")
    t_out = tok("out")

    small = [(identf, identf_d), (Umat, U_d), (negm, negm_d), (invf, invf_d), (cT, cT_d), (posi, pos_d),
             (badaT, badaT_d), (convw, convw_d), (convb, convb_d), (dtb, dtb_d), (alog, alog_d), (dsk, dsk_d),
             (ssdnw, ssdnw_d), (subln, subln_d), (dlam, dlam_d)]
    P.dma("sp", c_const, [((a if a is dlam else a[:]), b) for a, b in small], writes=[t_const])
    V = lambda f, **k: P.op("dve", f, **k)
    A = lambda f, **k: P.op("act", f, **k)
    G = lambda f, **k: P.op("dve", f, **k)
    V(lambda e: e.memset(onesf[:], 1.0), writes=[t_const])
    V(lambda e: e.memset(epsT[:], EPS), writes=[t_const])
    V(lambda e: e.tensor_copy(out=identb[:], in_=identf[:]), reads=[t_const], writes=[t_const])
    A(lambda e: e.activation(out=scT[:], in_=cT[:], func=AF.Silu), reads=[t_const], writes=[t_const])
    A(lambda e: e.activation(out=Aneg[:], in_=alog[:], func=AF.Exp), reads=[t_const], writes=[t_const])
    V(lambda e: e.tensor_scalar(out=Aneg[:], in0=Aneg[:], scalar1=-1.0, scalar2=None, op0=ALU.mult), reads=[t_const], writes=[t_const])
    for l in range(DEPTH):
        for i in range(2):
            V(lambda e, l=l, i=i: e.tensor_tensor(out=tmp5[0][:, 0:64], in0=dlam[:, l, 2 * i, :], in1=dlam[:, l, 2 * i + 1, :], op=ALU.mult),
              reads=[t_const], writes=[t_tmp5[0]])
            V(lambda e, l=l, i=i: e.tensor_reduce(out=lamt[:, l, i:i + 1], in_=tmp5[0][:, 0:64], axis=mybir.AxisListType.X, op=ALU.add),
              reads=[t_tmp5[0]], writes=[t_const])
    A(lambda e: e.activation(out=lamt[:], in_=lamt[:], func=AF.Exp), reads=[t_const], writes=[t_const])
    for l in range(DEPTH):
        lam_init = 0.8 - 0.6 * math.exp(-0.3 * l)
        V(lambda e, l=l, li=lam_init: e.scalar_tensor_tensor(out=nlam[:, l:l + 1], in0=lamt[:, l, 0:1], scalar=-1.0, in1=lamt[:, l, 1:2], op0=ALU.mult, op1=ALU.add),
          reads=[t_const], writes=[t_const])
        V(lambda e, l=l, li=lam_init: e.tensor_scalar(out=nlam[:, l:l + 1], in0=nlam[:, l:l + 1], scalar1=-li, scalar2=None, op0=ALU.add),
          reads=[t_const], writes=[t_const])
        V(lambda e, l=l, li=lam_init: e.tensor_scalar(out=sublnS[:, l:l + 1], in0=subln[:, l:l + 1], scalar1=(1.0 - li), scalar2=None, op0=ALU.mult),
          reads=[t_const], writes=[t_const])

    wada_v = [wada_d[l].rearrange("(kc p) c -> p kc c", p=128) for l in range(DEPTH)]

    def mod_slices(l, sl0, sl1, ffn_ring):
        for sl in range(sl0, sl1):
            s_ = next_w(ffn=ffn_ring)
            P.dma("pool", s_.clk_sw, [(s_.ap[:], wada_v[l][:, :, sl * 512:(sl + 1) * 512])], writes=[s_.tok])
            fns = []
            for mc in range(4):
                for kc in range(8):
                    fns.append(lambda e, s_=s_, mc=mc, kc=kc: e.matmul(
                        out=banks[0][:, 2 * mc:2 * mc + 2], lhsT=s_.ap[:, kc, mc * 128:(mc + 1) * 128], rhs=scT[:, kc, :],
                        start=(kc == 0), stop=(kc == 7)))
            P.group("pe", fns, reads=[s_.tok, t_const], writes=[btok[0]], same_ok=True)
            V(lambda e, l=l, sl=sl: e.tensor_tensor(out=modT[:, l, 4 * sl:4 * sl + 4, :], in0=banks[0][:, 0:8].rearrange("p (m s) -> p m s", s=2),
                                                    in1=badaT[:, l, 4 * sl:4 * sl + 4].unsqueeze(2).broadcast_to([128, 4, 2]), op=ALU.add),
              reads=[btok[0], t_const], writes=[t_modl[l]])

    mod_slices(0, 0, 18, False)

    t_wbf = {}

    pc_clks = [P.new_clock(f"pc{i}") for i in range(8)]

    def precast(l):
        for f in range(2):
            tk = [tok(f"wbf{l}{f}_{i}") for i in range(8)]
            t_wbf[(l, f)] = tk
            src = wfi_d[l, f].rearrange("(kc p) c -> p kc c", p=128)
            pairs = []
            for jg in range(NJG):
                for gu in range(2):
                    pairs.append((wbfi[l, f, jg, :, :, gu, :], src[:, :, gu * DFF + jg * 256: gu * DFF + (jg + 1) * 256]))
            src2 = wfo_d[l, f].rearrange("(j p) c -> p j c", p=128)
            for dmh in range(2):
                for fl in range(3):
                    nj = 8 if fl < 2 else 6
                    pairs.append((wbfo[l, f, dmh, fl, :, 0:nj, :], src2[:, 8 * fl:8 * fl + nj, dmh * 512:(dmh + 1) * 512]))
            if f == 0:
                P.dma_throttled("pool", pc_clks, pairs, writes=tk)
                tkm = [tok(f"wbm{l}_{i}") for i in range(8)]
                t_wbf[(l, "m")] = tkm
                srcm = win_d[l].rearrange("(kc p) c -> p kc c", p=128)
                grp_cols = [0, 512, 1024, 1544, 2056, 2568]
                pm = [(wbmi[l, g], srcm[:, :, c0:c0 + 512]) for g, c0 in enumerate(grp_cols)]
                pm.append((wbmd[l], srcm[:, :, 1536:1544]))
                srco = wout_d[l].rearrange("(kc p) c -> p kc c", p=128)
                for dmh in range(2):
                    pm.append((wbmo[l, dmh], srco[:, :, dmh * 512:(dmh + 1) * 512]))
                P.dma_throttled("pool", pc_clks, pm, writes=tkm)
            else:
                P.dma_throttled("pool", pc_clks, pairs, writes=tk)


    precast(0)

    def load_rows(l, sub):
        o = (l * 3 + sub) * D
        P.dma("sp", c_rows, [(lng_b[:], lng_d[:, o:o + D]), (lnb_b[:], lnb_d[:, o:o + D])],
              writes=[t_rows["lng"], t_rows["lnb"]])

    def sub_prep(l, sub, s, gate_mul):
        m0 = sub * 24
        V(lambda e: e.tensor_scalar(out=sc1[:], in0=modT[:, l, m0 + 8:m0 + 16, s], scalar1=1.0, scalar2=None, op0=ALU.add),
          reads=[t_modl[l]], writes=[t_sc])
        V(lambda e: e.tensor_scalar(out=gcol[:], in0=modT[:, l, m0 + 16:m0 + 24, s], scalar1=1.0, scalar2=gate_mul, op0=ALU.add, op1=ALU.mult),
          reads=[t_modl[l]], writes=[t_sc])
        for half in range(2):
            bk = 4 + half
            for q in range(4):
                kc = half * 4 + q
                i = kc % 2
                V(lambda e, kc=kc, i=i: e.tensor_scalar(out=dg[i][:], in0=identf[:], scalar1=gcol[:, kc:kc + 1], scalar2=None, op0=ALU.mult),
                  reads=[t_sc, t_const], writes=[t_dg[i]])
                P.group("pe", [lambda e, q=q, i=i, bk=bk: e.matmul(out=banks[bk][:, q * 128:(q + 1) * 128], lhsT=onesf[:], rhs=dg[i][:], start=True, stop=True)],
                        reads=[t_dg[i], t_const], writes=[btok[bk]], same_ok=True)
            A(lambda e, half=half, bk=bk: e.activation(out=gate_b[:, half * 512:(half + 1) * 512], in_=banks[bk][:], func=AF.Copy),
              reads=[btok[bk]], writes=[t_rows["gate"]])

    def phase_a_parts(l, sub, s, b, ub, xhats):
        m0 = sub * 24
        tiles = [4 * b + i for i in range(4)]

        def stats():
            for i, tt in enumerate(tiles):
                for hf in range(2):
                    V(lambda e, i=i, tt=tt, hf=hf: e.bn_stats(out=st6[:, i, hf, :], in_=X[:, tt, hf * 512:(hf + 1) * 512]),
                      reads=[Xtok[tt]], writes=[t_stat])
                V(lambda e, i=i: e.bn_aggr(out=mv[:, i, :], in_=st6[:, i, :, :].rearrange("p a b -> p (a b)")), reads=[t_stat], writes=[t_stat])
            A(lambda e: e.activation(out=rstd[:], in_=mv[:, :, 1], func=AF.Ln, bias=epsT[:, 0:1]), reads=[t_stat, t_const], writes=[t_stat])
            A(lambda e: e.activation(out=rstd[:], in_=rstd[:], func=AF.Exp, scale=-0.5), reads=[t_stat], writes=[t_stat])
            V(lambda e: e.scalar_tensor_tensor(out=nmr[:], in0=mv[:, :, 0], scalar=-1.0, in1=rstd[:], op0=ALU.mult, op1=ALU.mult),
              reads=[t_stat], writes=[t_stat])

        def mk_xhat(i):
            tt = tiles[i]
            xhat, xh_toks = xhats[i % len(xhats)]

            def f():
                A(lambda e: e.activation(out=xhat, in_=X[:, tt, :], func=AF.Identity, scale=rstd[:, i:i + 1], bias=nmr[:, i:i + 1]),
                  reads=[Xtok[tt], t_stat], writes=xh_toks)
            return f

        def mk_T(i):
            xhat, xh_toks = xhats[i % len(xhats)]
            b0 = (i % 2) * 2

            def f():
                for hb in range(2):
                    P.group("pe", [lambda e, q=q, hb=hb: e.transpose(out=banks[b0 + hb][:, q * 128:(q + 1) * 128],
                                                                     in_=xhat[:, (hb * 4 + q) * 128:(hb * 4 + q + 1) * 128], identity=identf[:])
                                   for q in range(4)],
                            reads=xh_toks + [t_const], writes=[btok[b0 + hb]], same_ok=True)
                for kc in range(8):
                    bk = b0 + kc // 4
                    q = kc % 4
                    if kc % 2 == 0:
                        A(lambda e, kc=kc, bk=bk, q=q: e.activation(out=uTb[ub][:, kc, i * 128:(i + 1) * 128], in_=banks[bk][:, q * 128:(q + 1) * 128],
                                                                     func=AF.Identity, scale=sc1[:, kc:kc + 1], bias=modT[:, l, m0 + kc:m0 + kc + 1, s]),
                          reads=[btok[bk], t_sc, t_modl[l]], writes=[t_uT[ub]])
                    else:
                        V(lambda e, kc=kc, bk=bk, q=q: e.tensor_scalar(out=uTb[ub][:, kc, i * 128:(i + 1) * 128], in0=banks[bk][:, q * 128:(q + 1) * 128],
                                                                        scalar1=sc1[:, kc:kc + 1], scalar2=modT[:, l, m0 + kc:m0 + kc + 1, s],
                                                                        op0=ALU.mult, op1=ALU.add),
                          reads=[btok[bk], t_sc, t_modl[l]], writes=[t_uT[ub]])
            return f

        return stats, [mk_xhat(i) for i in range(4)], [mk_T(i) for i in range(4)]

    def phase_a(l, sub, s, b, ub, xhats):
        st, xs_, ts_ = phase_a_parts(l, sub, s, b, ub, xhats)
        st()
        for i in range(4):
            xs_[i]()
            ts_[i]()

    def epilogue_parts(tt, bk, half, last):
        k = 2
        xs_ = X[:, tt, half * 512:(half + 1) * 512]

        def p1():
            V(lambda e: e.tensor_tensor(out=tmp5[k][:], in0=banks[bk][:], in1=gate_b[:, half * 512:(half + 1) * 512], op=ALU.mult),
              reads=[btok[bk], t_rows["gate"]], writes=[t_tmp5[k]])
            V(lambda e: e.scalar_tensor_tensor(out=xs_, in0=xs_, scalar=ALPHA, in1=tmp5[k][:], op0=ALU.mult, op1=ALU.add),
              reads=[t_tmp5[k], Xtok[tt]], writes=[Xtok[tt]])

        def p2():
            for hf in range(2):
                V(lambda e, hf=hf: e.bn_stats(out=est6[:, hf, :], in_=X[:, tt, hf * 512:(hf + 1) * 512]), reads=[Xtok[tt]], writes=[t_estat])
            V(lambda e: e.bn_aggr(out=emv[:], in_=est6[:].rearrange("p a b -> p (a b)")), reads=[t_estat], writes=[t_estat])
            A(lambda e: e.activation(out=erstd[:], in_=emv[:, 1:2], func=AF.Ln, bias=epsT[:, 0:1]), reads=[t_estat, t_const], writes=[t_estat])
            A(lambda e: e.activation(out=erstd[:], in_=erstd[:], func=AF.Exp, scale=-0.5), reads=[t_estat], writes=[t_estat])

        def p3():
            V(lambda e: e.scalar_tensor_tensor(out=enmr[:], in0=emv[:, 0:1], scalar=-1.0, in1=erstd[:], op0=ALU.mult, op1=ALU.mult),
              reads=[t_estat], writes=[t_estat])
            A(lambda e: e.activation(out=X[:, tt, :], in_=X[:, tt, :], func=AF.Identity, scale=erstd[:, 0:1], bias=enmr[:, 0:1]),
              reads=[t_estat, Xtok[tt]], writes=[Xtok[tt]])
            V(lambda e: e.tensor_tensor(out=X[:, tt, :], in0=X[:, tt, :], in1=lng_b[:], op=ALU.mult), reads=[Xtok[tt], t_rows["lng"]], writes=[Xtok[tt]])

        def p4():
            V(lambda e: e.tensor_tensor(out=X[:, tt, :], in0=X[:, tt, :], in1=lnb_b[:], op=ALU.add), reads=[Xtok[tt], t_rows["lnb"]], writes=[Xtok[tt]])

        return [p1, p2, p3, p4] if last else [p1]

    def epilogue(tt, bk, half, last):
        for p in epilogue_parts(tt, bk, half, last):
            p()

    epi_pending = []

    def ffn_block(l, f, s, b, ub, hook=None, pre=None):
        tkw = t_wbf[(l, f)]
        if pre is not None:
            pre()
        for jg in range(NJG):
            s_ = next_w(ffn=True)
            P.dma("sp", s_.clk, [(s_.ap[:].rearrange("p k c -> p (k c)"), wbfi[l, f, jg].rearrange("p k g c -> p (k g c)"))],
                  reads=list(tkw), writes=[s_.tok])
            if b == 0 and jg == 2:
                load_rows(l, 0 if f == 0 else 2)
            if jg == 1 and hook is not None:
                for h_ in hook.get("B", ()):
                    h_()
            wflat = s_.ap[:].rearrange("p k c -> p (k c)").rearrange("p (k g c) -> p k g c", k=8, g=2)
            DBG("w0", s_.ap[:], [s_.tok])
            DBG("uT", uTb[ub], [t_uT[ub]])
            for jj in range(2):
                j = 2 * jg + jj
                bg = (j % 2) * 2
                for gu in range(2):
                    P.group("pe", [lambda e, kc=kc, gu=gu, jj=jj, bg=bg: e.matmul(
                        out=banks[bg + gu][:], lhsT=wflat[:, kc, gu, jj * 128:(jj + 1) * 128], rhs=uTb[ub][:, kc, :],
                        start=(kc == 0), stop=(kc == 7)) for kc in range(8)],
                        reads=[s_.tok, t_uT[ub]], writes=[btok[bg + gu]], same_ok=True)
                k = j % 2
                A(lambda e, bg=bg, k=k: e.activation(out=tmp5[k][:], in_=banks[bg][:], func=AF.Silu), reads=[btok[bg]], writes=[t_tmp5[k]])
                V(lambda e, bg=bg, k=k, j=j: e.tensor_tensor(out=actT[:, j, :], in0=banks[bg + 1][:], in1=tmp5[k][:], op=ALU.mult),
                  reads=[btok[bg + 1], t_tmp5[k]], writes=[t_act])
                if epi_pending and j >= 1:
                    epi_pending.pop(0)()
        DBG("actT", actT[:, 0:16, :], [t_act])
        for dmh in range(2):
            for fl in range(3):
                nj = 8 if fl < 2 else 6
                so = next_w(ffn=True)
                P.dma("sp", so.clk, [(so.ap[:], wbfo[l, f, dmh, fl])], reads=list(tkw), writes=[so.tok])
                fns = []
                for jl in range(nj):
                    j = 8 * fl + jl
                    for i in range(4):
                        fns.append(lambda e, i=i, j=j, jl=jl, so=so: e.matmul(
                            out=banks[4 + i][:], lhsT=actT[:, j, i * 128:(i + 1) * 128], rhs=so.ap[:, jl, :],
                            start=(j == 0), stop=(j == NJ - 1)))
                P.group("pe", fns, reads=[so.tok, t_act], writes=[btok[4 + i] for i in range(4)], same_ok=True)
                if hook is not None:
                    for h_ in hook.get(dmh * 3 + fl, ()):
                        h_()
            for i in range(4):
                if dmh == 0:
                    epilogue(4 * b + i, 4 + i, dmh, False)
                else:
                    epi_pending.extend(epilogue_parts(4 * b + i, 4 + i, 1, True))

    def ffn_sublayer(l, f, s):
        sub = 0 if f == 0 else 2
        if f == 0 and s == 0 and l + 1 < nlayers:
            precast(l + 1)
        sub_prep(l, sub, s, 0.5)
        xh = [(xhat_f[0], [t_xhat_f[0]]), (xhat_f[1], [t_xhat_f[1]])]
        phase_a(l, sub, s, 0, 0, xh)
        for b in range(NB):
            nxt = None
            if b + 1 < NB:
                st, xs_, ts_ = phase_a_parts(l, sub, s, b + 1, (b + 1) % 2, xh)
                nxt = {0: [st, xs_[0], xs_[1]], 1: [ts_[0], xs_[2]], 2: [ts_[1], xs_[3]], 3: [ts_[2]], 4: [ts_[3]]}
            pre = None
            if f == 1 and s == 0 and l + 1 < nlayers:
                pre = (lambda b=b: mod_slices(l + 1, 5 * b, min(18, 5 * b + 5), True))
            ffn_block(l, f, s, b, b % 2, hook=nxt, pre=pre)
        while epi_pending:
            epi_pending.pop(0)()

    def rope_tables(s):
        posf = ytmp[:, 0:NT]
        ang = otmp[:, 0:NT * 8].rearrange("p (t i) -> p t i", i=8)
        kf = zs[:, 0:NT * 8].rearrange("p (t i) -> p t i", i=8)
        ki = xdt2[:, 0:2 * NT * 8].bitcast(I32).rearrange("p (t i) -> p t i", i=8)
        wr = qk_tm[1][:, 0:NT * 8].rearrange("p (t i) -> p t i", i=8)
        tk = tok("rope")
        V(lambda e: e.tensor_copy(out=posf, in_=posi[:, s, :]), reads=[t_const], writes=[tk])
        V(lambda e: e.tensor_tensor(out=ang, in0=posf.unsqueeze(2).broadcast_to([128, NT, 8]),
                                    in1=invf[:].unsqueeze(1).broadcast_to([128, NT, 8]), op=ALU.mult), reads=[tk, t_const], writes=[tk])
        V(lambda e: e.tensor_scalar(out=ki, in0=ang, scalar1=1.0 / TWO_PI, scalar2=None, op0=ALU.mult), reads=[tk], writes=[tk])
        V(lambda e: e.tensor_copy(out=kf, in_=ki), reads=[tk], writes=[tk])
        C1 = 6.28125
        C2 = TWO_PI - C1
        V(lambda e: e.scalar_tensor_tensor(out=ang, in0=kf, scalar=-C1, in1=ang, op0=ALU.mult, op1=ALU.add), reads=[tk], writes=[tk])
        V(lambda e: e.scalar_tensor_tensor(out=ang, in0=kf, scalar=-C2, in1=ang, op0=ALU.mult, op1=ALU.add), reads=[tk], writes=[tk])
        for dst, shift in ((sinT, 0.0), (cosT, math.pi / 2)):
            V(lambda e, dst=dst, shift=shift: e.tensor_scalar(out=dst[:], in0=ang, scalar1=float(shift), scalar2=None, op0=ALU.add), reads=[tk], writes=[tk])
            V(lambda e, dst=dst: e.tensor_scalar(out=wr, in0=dst[:], scalar1=math.pi, scalar2=-TWO_PI, op0=ALU.is_gt, op1=ALU.mult), reads=[tk], writes=[tk])
            V(lambda e, dst=dst: e.tensor_tensor(out=dst[:], in0=dst[:], in1=wr, op=ALU.add), reads=[tk], writes=[tk])
            V(lambda e, dst=dst: e.tensor_scalar(out=wr, in0=dst[:], scalar1=-math.pi, scalar2=TWO_PI, op0=ALU.is_lt, op1=ALU.mult), reads=[tk], writes=[tk])
            V(lambda e, dst=dst: e.tensor_tensor(out=dst[:], in0=dst[:], in1=wr, op=ALU.add), reads=[tk], writes=[tk])
            A(lambda e, dst=dst: e.activation(out=dst[:], in_=dst[:], func=AF.Sin), reads=[tk], writes=[tk])
        return tk

    rope_tok = [None]

    def mixer_sublayer(l, s):
        sub_prep(l, 1, s, 1.0)
        tkw = t_wbf[(l, "m")]
        if l == 0:
            rope_tok[0] = rope_tables(s)
        t_rope = rope_tok[0]
        V(lambda e: e.memset(hist, 0.0), writes=[tm["hist"]])
        V(lambda e: e.memset(prev, 0.0), writes=[tm["prev"]])
        V(lambda e: e.memset(prevb, 0.0), writes=[tm["prevb"]])
        V(lambda e: e.memset(Vc[:, :, :, 128:130], 1.0), writes=[tm["V"]])
        ptc = [0]
        hv = lambda ap: ap.rearrange("p (h q) -> p h q", q=64)
        for b in range(NB):
            ub = 0
            if b == 0:
                phase_a(l, 1, s, b, ub, [(xhat_m, [tm["ytmp"], tm["otmp"]])])
            uT = uTb[ub]
            sd = next_w()
            P.dma("sp", sd.clk, [(sd.ap[:, :, 0:8], wbmd[l])], reads=list(tkw), writes=[sd.tok])
            for i in range(4):
                P.group("pe", [lambda e, kc=kc, i=i, sd=sd: e.matmul(out=banks[7][:, i * 8:(i + 1) * 8], lhsT=uT[:, kc, i * 128:(i + 1) * 128],
                                                                      rhs=sd.ap[:, kc, 0:8], start=(kc == 0), stop=(kc == 7)) for kc in range(8)],
                        reads=[sd.tok, t_uT[ub]], writes=[btok[7]], same_ok=True)
            V(lambda e: e.tensor_tensor(out=dts, in0=banks[7][:, 0:32].rearrange("p (a h) -> p a h", h=8),
                                        in1=dtb[:, l, :].unsqueeze(1).broadcast_to([128, 4, 8]), op=ALU.add),
              reads=[btok[7], t_const], writes=[tm["dt"]])
            A(lambda e: e.activation(out=dts, in_=dts, func=AF.Exp), reads=[tm["dt"]], writes=[tm["dt"]])
            A(lambda e: e.activation(out=dts, in_=dts, func=AF.Ln, bias=1.0), reads=[tm["dt"]], writes=[tm["dt"]])
            V(lambda e: e.tensor_tensor(out=a_t, in0=dts, in1=Aneg[:, l, :].unsqueeze(1).broadcast_to([128, 4, 8]), op=ALU.mult),
              reads=[tm["dt"], t_const], writes=[tm["dt"]])
            a_flat = a_t.rearrange("p a h -> p (a h)")
            P.group("pe", [lambda e: e.matmul(out=banks[7][:, 64:96], lhsT=Umat[:], rhs=a_flat, start=True, stop=True),
                           lambda e: e.matmul(out=banks[7][:, 128:160], lhsT=onesf[:], rhs=a_flat, start=True, stop=True)],
                    reads=[tm["dt"], t_const], writes=[btok[7]], same_ok=True)
            acs_p = banks[7][:, 64:96].rearrange("p (a h) -> p a h", h=8)
            tot_p = banks[7][:, 128:160].rearrange("p (a h) -> p a h", h=8)
            V(lambda e: e.tensor_copy(out=acs, in_=acs_p), reads=[btok[7]], writes=[tm["dt"]])
            V(lambda e: e.tensor_scalar(out=nacs, in0=acs_p, scalar1=-1.0, scalar2=None, op0=ALU.mult), reads=[btok[7]], writes=[tm["dt"]])
            V(lambda e: e.tensor_tensor(out=dte, in0=tot_p, in1=acs, op=ALU.subtract), reads=[btok[7], tm["dt"]], writes=[tm["dt"]])
            A(lambda e: e.activation(out=dte, in_=dte, func=AF.Exp), reads=[tm["dt"]], writes=[tm["dt"]])
            A(lambda e: e.activation(out=ea, in_=acs, func=AF.Exp), reads=[tm["dt"]], writes=[tm["dt"]])
            A(lambda e: e.activation(out=cd, in_=tot_p, func=AF.Exp), reads=[btok[7]], writes=[tm["dt"]])
            V(lambda e: e.tensor_tensor(out=dtd, in0=dts, in1=dte, op=ALU.mult), reads=[tm["dt"]], writes=[tm["dt"]])

            conv_pending = []
            for grp in range(2):
                s_ = next_w()
                P.dma("sp", s_.clk, [(s_.ap[:], wbmi[l, 1 + grp])], reads=list(tkw), writes=[s_.tok])
                for c4 in range(4):
                    cc = grp * 4 + c4
                    bk = 4 + cc % 2
                    P.group("pe", [lambda e, kc=kc, c4=c4, bk=bk, s_=s_: e.matmul(out=banks[bk][:], lhsT=s_.ap[:, kc, c4 * 128:(c4 + 1) * 128],
                                                                               rhs=uT[:, kc, :], start=(kc == 0), stop=(kc == 7)) for kc in range(8)],
                            reads=[s_.tok, t_uT[ub]], writes=[btok[bk]], same_ok=True)
                    if len(conv_pending) > 0:
                        conv_pending.pop(0)()
                    sg = stag[cc % 2]
                    tsg = tm[f"stag{cc % 2}"]
                    A(lambda e, sg=sg, bk=bk: e.activation(out=sg[:, 3:515], in_=banks[bk][:], func=AF.Copy), reads=[btok[bk]], writes=[tsg])
                    V(lambda e, sg=sg, cc=cc: e.tensor_copy(out=sg[:, 0:3], in_=hist[:, cc, 0:3]), reads=[tm["hist"]], writes=[tsg])
                    V(lambda e, sg=sg, cc=cc: e.tensor_copy(out=hist[:, cc, 0:3], in_=sg[:, 512:515]), reads=[tsg], writes=[tm["hist"]])
                    k = cc % 2
                    co = tmp5[k]
                    tco = t_tmp5[k]
                    A(lambda e, sg=sg, cc=cc, co=co: e.activation(out=co[:], in_=sg[:, 0:512], func=AF.Identity, scale=convw[:, l, 0, cc:cc + 1],
                                                                  bias=convb[:, l, cc:cc + 1]), reads=[tsg, t_const], writes=[tco])
                    for tap in range(1, 4):
                        V(lambda e, sg=sg, cc=cc, co=co, tap=tap: e.scalar_tensor_tensor(out=co[:], in0=sg[:, tap:tap + 512], scalar=convw[:, l, tap, cc:cc + 1],
                                                                                         in1=co[:], op0=ALU.mult, op1=ALU.add),
                          reads=[tsg, t_const, tco], writes=[tco])
                    if cc < 4:
                        A(lambda e, co=co: e.activation(out=co[:], in_=co[:], func=AF.Silu), reads=[tco], writes=[tco])

                        def fin(cc=cc, co=co, tco=tco):
                            P.group("pe", [lambda e, i=i, co=co: e.transpose(out=banks[6][:, i * 128:(i + 1) * 128], in_=co[:, i * 128:(i + 1) * 128], identity=identf[:])
                                           for i in range(4)], reads=[tco, t_const], writes=[btok[6]], same_ok=True)
                            V(lambda e, cc=cc: e.tensor_copy(out=xs_tm[:, :, cc * 128:(cc + 1) * 128], in_=banks[6][:].rearrange("p (a c) -> p a c", c=128)),
                              reads=[btok[6]], writes=[tm["xs"]])
                        conv_pending.append(fin)
                    else:
                        A(lambda e, co=co, cc=cc: e.activation(out=BCT[:, cc - 4, :], in_=co[:], func=AF.Silu), reads=[tco], writes=[tm["BCT"]])
                        if cc < 6:
                            def fin(cc=cc):
                                bt = banks[6][:].bitcast(BF16)
                                P.group("pe", [lambda e, i=i, cc=cc, bt=bt: e.transpose(out=bt[:, i * 128:(i + 1) * 128], in_=BCT[:, cc - 4, i * 128:(i + 1) * 128], identity=identb[:])
                                               for i in range(4)], reads=[tm["BCT"], t_const], writes=[btok[6]], same_ok=True)
                                V(lambda e, cc=cc, bt=bt: e.tensor_copy(out=Btm[:, :, (cc - 4) * 128:(cc - 3) * 128], in_=bt[:, 0:512].rearrange("p (a c) -> p a c", c=128)),
                                  reads=[btok[6]], writes=[tm["Btm"]])
                            conv_pending.append(fin)
            while conv_pending:
                conv_pending.pop(0)()
            V(lambda e: e.tensor_tensor(out=xdt.rearrange("p a (h q) -> p a h q", q=64), in0=xs_tm.rearrange("p a (h q) -> p a h q", q=64),
                                        in1=dts.unsqueeze(3).broadcast_to([128, 4, 8, 64]), op=ALU.mult),
              reads=[tm["xs"], tm["dt"]], writes=[tm["xdt"]])

            sv = next_w(); P.dma("sp", sv.clk, [(sv.ap[:], wbmi[l, 5])], reads=list(tkw), writes=[sv.tok])
            if b == 0:
                load_rows(l, 1)
            for i in range(4):
                tt = 4 * b + i
                bk = 4 + i % 2
                P.group("pe", [lambda e, kc=kc, i=i, bk=bk: e.matmul(out=banks[bk][:], lhsT=uT[:, kc, i * 128:(i + 1) * 128], rhs=sv.ap[:, kc, :],
                                                                      start=(kc == 0), stop=(kc == 7)) for kc in range(8)],
                        reads=[sv.tok, t_uT[ub]], writes=[btok[bk]], same_ok=True)
                V(lambda e, tt=tt, bk=bk: e.tensor_copy(out=Vc[:, tt, :, 0:128], in_=banks[bk][:].rearrange("p (h c) -> p h c", c=128)),
                  reads=[btok[bk]], writes=[tm["V"]])

            sq = next_w(); P.dma("sp", sq.clk, [(sq.ap[:], wbmi[l, 3])], reads=list(tkw), writes=[sq.tok])
            sk = next_w(); P.dma("sp", sk.clk, [(sk.ap[:], wbmi[l, 4])], reads=list(tkw), writes=[sk.tok])
            def qk_proj(i):
                tt = 4 * b + i
                for wi, (which, sw) in enumerate((("q", sq), ("k", sk))):
                    bk = 4 + wi
                    P.group("pe", [lambda e, kc=kc, i=i, bk=bk, sw=sw: e.matmul(out=banks[bk][:], lhsT=uT[:, kc, i * 128:(i + 1) * 128], rhs=sw.ap[:, kc, :],
                                                                             start=(kc == 0), stop=(kc == 7)) for kc in range(8)],
                            reads=[sw.tok, t_uT[ub]], writes=[btok[bk]], same_ok=True)
                for wi in range(2):
                    bk = 4 + wi
                    qk = qk_tm[wi]
                    tq = tm[f"qk{wi}"]
                    A(lambda e, qk=qk, bk=bk: e.activation(out=qk, in_=banks[bk][:], func=AF.Copy), reads=[btok[bk]], writes=[tq])
                for wi in range(2):
                    qk = qk_tm[wi]
                    tq = tm[f"qk{wi}"]
                    qv = qk.rearrange("p (g d) -> p g d", d=64)
                    t1 = qv[:, :, 0:8]
                    t2 = qv[:, :, 8:16]
                    cb = cosT[:, tt, :].unsqueeze(1).broadcast_to([128, 8, 8])
                    sbb = sinT[:, tt, :].unsqueeze(1).broadcast_to([128, 8, 8])
                    rd = [tq, t_rope]
                    V(lambda e, t1=t1, cb=cb: e.tensor_tensor(out=ropet[:, :, 0, :], in0=t1, in1=cb, op=ALU.mult), reads=rd, writes=[tm["rope"]])
                    V(lambda e, t2=t2, sbb=sbb: e.tensor_tensor(out=ropet[:, :, 1, :], in0=t2, in1=sbb, op=ALU.mult), reads=rd, writes=[tm["rope"]])
                    V(lambda e, t2=t2, cb=cb: e.tensor_tensor(out=ropet[:, :, 2, :], in0=t2, in1=cb, op=ALU.mult), reads=rd, writes=[tm["rope"]])
                    V(lambda e, t1=t1, sbb=sbb: e.tensor_tensor(out=ropet[:, :, 3, :], in0=t1, in1=sbb, op=ALU.mult), reads=rd, writes=[tm["rope"]])
                    V(lambda e, t1=t1: e.tensor_tensor(out=t1, in0=ropet[:, :, 0, :], in1=ropet[:, :, 1, :], op=ALU.subtract), reads=[tm["rope"]], writes=[tq])
                    V(lambda e, t2=t2: e.tensor_tensor(out=t2, in0=ropet[:, :, 2, :], in1=ropet[:, :, 3, :], op=ALU.add), reads=[tm["rope"]], writes=[tq])

            def qk_T(i):
                tt = 4 * b + i
                QTi = QT2[i % 2]
                tQT = tm[f"QT{i % 2}"]
                for wi, which in enumerate(("q", "k")):
                    bk = 4 + wi
                    qk = qk_tm[wi]
                    tq = tm[f"qk{wi}"]
                    P.group("pe", [lambda e, h=h, qk=qk, bk=bk: e.transpose(out=banks[bk][:, h * 128:(h + 1) * 128], in_=qk[:, h * 128:(h + 1) * 128], identity=identf[:])
                                   for h in range(4)], reads=[tq, t_const], writes=[btok[bk]], same_ok=True)
                    if which == "q":
                        A(lambda e, bk=bk, QTi=QTi: e.activation(out=QTi, in_=banks[bk][:].rearrange("p (h c) -> p h c", c=128), func=AF.Copy, scale=0.125),
                          reads=[btok[bk]], writes=[tQT])
                    else:
                        A(lambda e, tt=tt, bk=bk: e.activation(out=KT[:, :, tt * 128:(tt + 1) * 128], in_=banks[bk][:].rearrange("p (h c) -> p h c", c=128), func=AF.Copy),
                          reads=[btok[bk]], writes=[tm["KT"]])

            def qk_prep(i):
                qk_proj(i)
                qk_T(i)

            def attn(i):
                qt = 4 * b + i
                QTi = QT2[i % 2]
                tQT = tm[f"QT{i % 2}"]
                nk = qt + 1
                groups = []
                for h in range(4):
                    ngrp = (nk + 3) // 4
                    for gi in range(ngrp):
                        groups.append((h, list(range(gi * 4, min(nk, gi * 4 + 4))), gi == 0, gi == ngrp - 1))

                def att_s1(n):
                    h, kts, first, last = groups[n]
                    sA = (0, 6, 4)[n % 3]
                    pi = n % 3
                    pt = PTr[pi]
                    tpt = tm[f"pt{pi}"]
                    nn = len(kts)
                    fns = []
                    for n_, kt in enumerate(kts):
                        for j in range(2):
                            ps = slice(64 * j, 64 * j + 64)
                            fns.append(lambda e, kt=kt, n_=n_, h=h, ps=ps, bk=sA + j: e.matmul(
                                out=banks[bk][:, n_ * 128:(n_ + 1) * 128], lhsT=KT[ps, h, kt * 128:(kt + 1) * 128], rhs=QTi[ps, h, :], start=True, stop=True))
                    P.group("pe", fns, reads=[tm["KT"], tQT], writes=[btok[sA], btok[sA + 1]], same_ok=True)
                    for j in range(2):
                        A(lambda e, pt=pt, bk=sA + j, nn=nn, j=j: e.activation(out=pt[:, j, 0:nn, :], in_=banks[bk][:, 0:nn * 128].rearrange("p (a c) -> p a c", c=128), func=AF.Exp),
                          reads=[btok[sA + j]], writes=[tpt])
                    if kts[-1] == qt:
                        V(lambda e, pt=pt, nn=nn: e.memset(pt[64:128, :, nn - 1, 0:64], 0.0), reads=[], writes=[tpt])

                def att_s2(n):
                    h, kts, first, last = groups[n]
                    pi = n % 3
                    pt = PTr[pi]
                    tpt = tm[f"pt{pi}"]
                    a0 = 2 + (h % 2)
                    fns = []
                    for j in range(2):
                        for n_, kt in enumerate(kts):
                            fns.append(lambda e, kt=kt, n_=n_, h=h, pt=pt, a0=a0, nk=nk, j=j, st=(first and j == 0 and n_ == 0): e.matmul(
                                out=banks[a0][:, j * 256:j * 256 + 129], lhsT=pt[:, j, n_, :], rhs=Vc[:, kt, h, 0:129], start=st, stop=(kt == nk - 1),
                                skip_group_check=True))
                    P.group("pe", fns, reads=[tpt, tm["V"]], writes=[btok[a0]], same_ok=True)
                    if last:
                        V(lambda e, a0=a0: e.reciprocal(out=rr[:, 4:5], in_=banks[a0][:, 128:129]), reads=[btok[a0]], writes=[tm["ss"]])
                        V(lambda e, a0=a0: e.reciprocal(out=rr[:, 5:6], in_=banks[a0][:, 384:385]), reads=[btok[a0]], writes=[tm["ss"]])
                        V(lambda e: e.tensor_tensor(out=rr[:, 5:6], in0=rr[:, 5:6], in1=nlam[:, l:l + 1], op=ALU.mult), reads=[tm["ss"], t_const], writes=[tm["ss"]])
                        V(lambda e, h=h, a0=a0: e.tensor_scalar(out=otmp[:, h * 128:(h + 1) * 128], in0=banks[a0][:, 0:128], scalar1=rr[:, 4:5], scalar2=None, op0=ALU.mult),
                          reads=[btok[a0], tm["ss"]], writes=[tm["otmp"]])
                        V(lambda e, h=h, a0=a0: e.scalar_tensor_tensor(out=otmp[:, h * 128:(h + 1) * 128], in0=banks[a0][:, 256:384], scalar=rr[:, 5:6],
                                                                       in1=otmp[:, h * 128:(h + 1) * 128], op0=ALU.mult, op1=ALU.add),
                          reads=[btok[a0], tm["ss"], tm["otmp"]], writes=[tm["otmp"]])

                for n in range(len(groups) + ATT_LA):
                    if n < len(groups):
                        att_s1(n)
                    if n - ATT_LA >= 0:
                        att_s2(n - ATT_LA)

            def o_norm(i):
                for h in range(4):
                    A(lambda e, h=h: e.activation(out=zs[:, h * 128:(h + 1) * 128], in_=otmp[:, h * 128:(h + 1) * 128], func=AF.Square, accum_out=ss[:, 4 + h:5 + h]),
                      reads=[tm["otmp"]], writes=[tm["zs"], tm["ss"]])
                A(lambda e: e.activation(out=rr[:, 0:4], in_=ss[:, 4:8], func=AF.Ln, scale=1.0 / 128, bias=epsT[:, 0:1]), reads=[tm["ss"], t_const], writes=[tm["ss"]])
                A(lambda e: e.activation(out=rr[:, 0:4], in_=rr[:, 0:4], func=AF.Exp, scale=-0.5), reads=[tm["ss"]], writes=[tm["ss"]])
                for h in range(4):
                    A(lambda e, h=h: e.activation(out=otmp[:, h * 128:(h + 1) * 128], in_=otmp[:, h * 128:(h + 1) * 128], func=AF.Copy, scale=rr[:, h:h + 1]),
                      reads=[tm["ss"], tm["otmp"]], writes=[tm["otmp"]])

            def o_T(i):
                P.group("pe", [lambda e, c=c: e.transpose(out=banks[5][:, c * 128:(c + 1) * 128], in_=otmp[:, c * 128:(c + 1) * 128], identity=identf[:]) for c in range(4)],
                        reads=[tm["otmp"], t_const], writes=[btok[5]], same_ok=True)
                A(lambda e, i=i: e.activation(out=yoT[:, 4:8, i * 128:(i + 1) * 128], in_=banks[5][:].rearrange("p (h c) -> p h c", c=128), func=AF.Copy, scale=sublnS[:, l:l + 1]),
                  reads=[btok[5], t_const], writes=[tm["yoT"]])

            qk_prep(0)
            qk_prep(1)
            for i in range(4):
                attn(i)
                if i + 2 < 4:
                    qk_proj(i + 2)
                o_norm(i)
                if i + 2 < 4:
                    qk_T(i + 2)
                o_T(i)

            sz = next_w(); P.dma("sp", sz.clk, [(sz.ap[:], wbmi[l, 0])], reads=list(tkw), writes=[sz.tok])
            def ssd_a0(i):
                for hg in range(2):
                    G(lambda e, i=i, hg=hg: e.tensor_tensor(out=aU2[hg], in0=Umat[:].unsqueeze(1).broadcast_to([128, 4, 128]),
                                                            in1=a_t[:, i, 4 * hg:4 * hg + 4].unsqueeze(2).broadcast_to([128, 4, 128]), op=ALU.mult),
                      reads=[tm["dt"], t_const], writes=[tm[f"aU{hg}"]])

            def ssd_a1(i):
                tsl = slice(i * 128, (i + 1) * 128)
                P.group("pe", [lambda e, g=g, tsl=tsl: e.matmul(out=banks[2][:, g * 128:(g + 1) * 128], lhsT=BCT[:, g, tsl], rhs=BCT[:, 2 + g, tsl], start=True, stop=True)
                               for g in range(2)], reads=[tm["BCT"]], writes=[btok[2]], same_ok=True)
                for hg in range(2):
                    P.group("pe", [lambda e, hg=hg: e.matmul(out=banks[hg][:], lhsT=onesf[:], rhs=aU2[hg].rearrange("p a b -> p (a b)"), start=True, stop=False),
                                   lambda e, hg=hg: e.matmul(out=banks[hg][:], lhsT=identf[:], rhs=negm[:], start=False, stop=True)],
                            reads=[tm[f"aU{hg}"], t_const], writes=[btok[hg]], same_ok=True)
                for hg in range(2):
                    for hh in range(4):
                        h = 4 * hg + hh
                        A(lambda e, h=h, hh=hh, hg=hg, i=i: e.activation(out=Eb2[hg][:, hh, :], in_=banks[hg][:, hh * 128:(hh + 1) * 128], func=AF.Exp,
                                                                       bias=nacs[:, i, h:h + 1]), reads=[btok[hg], tm["dt"]], writes=[tm[f"E{hg}"]])

            def ssd_a2(i):
                for hg in range(2):
                    V(lambda e, hg=hg: e.tensor_tensor(out=Mb[:, 4 * hg:4 * hg + 4, :], in0=banks[2][:, hg * 128:(hg + 1) * 128].unsqueeze(1).broadcast_to([128, 4, 128]),
                                                       in1=Eb2[hg], op=ALU.mult), reads=[btok[2], tm[f"E{hg}"]], writes=[tm["M"]])

            def ssd_b1(i):
                tsl = slice(i * 128, (i + 1) * 128)
                P.group("pe", [lambda e, h=h, i=i: e.matmul(out=banks[3][:, h * 64:(h + 1) * 64], lhsT=Mb[:, h, :], rhs=xdt[:, i, h * 64:(h + 1) * 64], start=True, stop=True)
                               for h in range(8)], reads=[tm["M"], tm["xdt"]], writes=[btok[3]], same_ok=True)
                P.group("pe", [lambda e, g=g, tsl=tsl: e.matmul(out=banks[4][:, g * 256:(g + 1) * 256], lhsT=BCT[:, 2 + g, tsl], rhs=prevb[:, g * 256:(g + 1) * 256], start=True, stop=True)
                               for g in range(2)], reads=[tm["BCT"], tm["prevb"]], writes=[btok[4]], same_ok=True)

            def ssd_b2a(i):
                G(lambda e, i=i: e.tensor_tensor(out=hv(xdt2), in0=hv(xs_tm[:, i, :]), in1=dtd[:, i, :].unsqueeze(2).broadcast_to([128, 8, 64]), op=ALU.mult),
                  reads=[tm["xs"], tm["dt"]], writes=[tm["xdt2"]])
                P.group("pe", [lambda e, g=g, i=i: e.matmul(out=banks[5][:, g * 256:(g + 1) * 256], lhsT=Btm[:, i, g * 128:(g + 1) * 128], rhs=xdt2[:, g * 256:(g + 1) * 256], start=True, stop=True)
                               for g in range(2)], reads=[tm["Btm"], tm["xdt2"]], writes=[btok[5]], same_ok=True)
                P.group("pe", [lambda e, kc=kc, i=i: e.matmul(out=banks[6][:], lhsT=uT[:, kc, i * 128:(i + 1) * 128], rhs=sz.ap[:, kc, :], start=(kc == 0), stop=(kc == 7))
                               for kc in range(8)], reads=[sz.tok, t_uT[ub]], writes=[btok[6]], same_ok=True)
                G(lambda e, i=i: e.tensor_tensor(out=hv(prev), in0=hv(prev), in1=cd[:, i, :].unsqueeze(2).broadcast_to([128, 8, 64]), op=ALU.mult),
                  reads=[tm["dt"], tm["prev"]], writes=[tm["prev"]])
                V(lambda e: e.tensor_tensor(out=prev, in0=banks[5][:], in1=prev, op=ALU.add), reads=[btok[5], tm["prev"]], writes=[tm["prev"]])
                A(lambda e: e.activation(out=prevb, in_=prev, func=AF.Copy), reads=[tm["prev"]], writes=[tm["prevb"]])
                A(lambda e: e.activation(out=zs, in_=banks[6][:], func=AF.Silu), reads=[btok[6]], writes=[tm["zs"]])

            def ssd_b2b(i):
                V(lambda e, i=i: e.tensor_tensor(out=hv(ytmp), in0=hv(banks[4][:]), in1=ea[:, i, :].unsqueeze(2).broadcast_to([128, 8, 64]), op=ALU.mult),
                  reads=[btok[4], tm["dt"]], writes=[tm["ytmp"]])
                V(lambda e: e.tensor_tensor(out=ytmp, in0=banks[3][:], in1=ytmp, op=ALU.add), reads=[btok[3], tm["ytmp"]], writes=[tm["ytmp"]])
                G(lambda e, i=i: e.tensor_tensor(out=hv(otmp), in0=hv(xs_tm[:, i, :]), in1=dsk[:, l, :].unsqueeze(2).broadcast_to([128, 8, 64]), op=ALU.mult),
                  reads=[tm["xs"], t_const], writes=[tm["otmp"]])
                V(lambda e: e.tensor_tensor(out=ytmp, in0=ytmp, in1=otmp, op=ALU.add), reads=[tm["otmp"], tm["ytmp"]], writes=[tm["ytmp"]])
                V(lambda e: e.tensor_tensor(out=ytmp, in0=ytmp, in1=zs, op=ALU.mult), reads=[tm["zs"], tm["ytmp"]], writes=[tm["ytmp"]])
                for g in range(2):
                    A(lambda e, g=g: e.activation(out=zs[:, g * 256:(g + 1) * 256], in_=ytmp[:, g * 256:(g + 1) * 256], func=AF.Square, accum_out=ss[:, g:g + 1]),
                      reads=[tm["ytmp"]], writes=[tm["zs"], tm["ss"]])
                A(lambda e: e.activation(out=rr[:, 0:2], in_=ss[:, 0:2], func=AF.Ln, scale=1.0 / 256, bias=epsT[:, 0:1]), reads=[tm["ss"], t_const], writes=[tm["ss"]])
                A(lambda e: e.activation(out=rr[:, 0:2], in_=rr[:, 0:2], func=AF.Exp, scale=-0.5), reads=[tm["ss"]], writes=[tm["ss"]])
                for g in range(2):
                    A(lambda e, g=g: e.activation(out=ytmp[:, g * 256:(g + 1) * 256], in_=ytmp[:, g * 256:(g + 1) * 256], func=AF.Copy, scale=rr[:, g:g + 1]),
                      reads=[tm["ss"], tm["ytmp"]], writes=[tm["ytmp"]])

            def ssd_yT(i):
                tsl = slice(i * 128, (i + 1) * 128)
                P.group("pe", [lambda e, c=c: e.transpose(out=banks[7][:, c * 128:(c + 1) * 128], in_=ytmp[:, c * 128:(c + 1) * 128], identity=identf[:]) for c in range(4)],
                        reads=[tm["ytmp"], t_const], writes=[btok[7]], same_ok=True)
                for c in range(4):
                    A(lambda e, c=c, tsl=tsl: e.activation(out=yoT[:, c, tsl], in_=banks[7][:, c * 128:(c + 1) * 128], func=AF.Copy, scale=ssdnw[:, l, c:c + 1]),
                      reads=[btok[7], t_const], writes=[tm["yoT"]])

            pa_next = None
            if b + 1 < NB:
                pa_next = phase_a_parts(l, 1, s, b + 1, ub, [(xhat_m, [tm["ytmp"], tm["otmp"]])])
                pa_next[0]()
            ssd_a0(0)
            ssd_a1(0)
            ssd_a2(0)
            ssd_a0(1)
            for i in range(4):
                if i + 1 < 4:
                    ssd_a1(i + 1)
                ssd_b1(i)
                if i + 1 < 4:
                    ssd_a2(i + 1)
                if i + 2 < 4:
                    ssd_a0(i + 2)
                ssd_b2a(i)
                if i >= 1:
                    ssd_yT(i - 1)
                ssd_b2b(i)
            ssd_yT(3)

            so = [None, None]
            for dmh in range(2):
                so[dmh] = next_w()
                P.dma("sp", so[dmh].clk, [(so[dmh].ap[:], wbmo[l, dmh])], reads=list(tkw), writes=[so[dmh].tok])
            if pa_next is not None:
                pa_next[1][0]()
            for i in range(4):
                for dmh in range(2):
                    bk = 4 + (i * 2 + dmh) % 4
                    P.group("pe", [lambda e, kc=kc, i=i, dmh=dmh, bk=bk: e.matmul(out=banks[bk][:], lhsT=yoT[:, kc, i * 128:(i + 1) * 128], rhs=so[dmh].ap[:, kc, :],
                                                                               start=(kc == 0), stop=(kc == 7)) for kc in range(8)],
                            reads=[so[dmh].tok, tm["yoT"]], writes=[btok[bk]], same_ok=True)
                    if dmh == 0 and pa_next is not None:
                        pa_next[2][i]()
                        if i + 1 < 4:
                            pa_next[1][i + 1]()
                    epilogue(4 * b + i, bk, dmh, dmh == 1)

    done = False
    try:
      for s in range(1 if dbg else 0):
        for b in range(NB):
            P.dma("sp", c_x[b], [(X[:, 4 * b + i, :], x_d[s, (4 * b + i) * 128:(4 * b + i + 1) * 128, :]) for i in range(4)],
                  writes=[Xtok[4 * b + i] for i in range(4)])
        if dbg:
            for l in range(nlayers):
                for sub in range(3):
                    P.barrier()
                    if sub == 1:
                        mixer_sublayer(l, s)
                    else:
                        ffn_sublayer(l, 0 if sub == 0 else 1, s)
    except StopBuild:
        P.emit()
        return nc, P
    if dbg:
        raise RuntimeError("dbg point not reached")
    P2 = None
    for s in range(nseq):
        for b in range(NB):
            P.dma("sp", c_x[b], [(X[:, 4 * b + i, :], x_d[s, (4 * b + i) * 128:(4 * b + i + 1) * 128, :]) for i in range(4)],
                  writes=[Xtok[4 * b + i] for i in range(4)])
        for l in range(nlayers):
            for sub in range(3):
                if sub != 0 or (l == 0 and s == 0):
                    P.barrier()
                if sub == 1:
                    mixer_sublayer(l, s)
                else:
                    ffn_sublayer(l, 0 if sub == 0 else 1, s)
                if stop is not None and (l, sub) == tuple(stop):
                    done = True
                    break
            if done:
                break
        for tt in range(NT):
            P.dma("sp", c_out, [(out_d[s, tt * 128:(tt + 1) * 128, :], X[:, tt, :])], reads=[Xtok[tt]], writes=[t_out])
        done = False
    P.finish([t_out])
    P.emit()
    return nc, P


_CACHE = {}


def _consts():
    identf = np.eye(128, dtype=np.float32)
    U = np.triu(np.ones((128, 128), dtype=np.float32))
    neg = np.where(np.arange(128)[None, :] < np.arange(128)[:, None], np.float32(NEG), np.float32(0.0)).astype(np.float32)
    negmask4 = np.tile(neg, (1, 4)).astype(np.float32)
    inv = np.array([500000.0 ** (-(i * 2.0) / 16.0) for i in range(8)], dtype=np.float32)
    invf = np.broadcast_to(inv[None, :], (128, 8)).copy()
    return identf, U, negmask4, invf


def _rep(a):
    return np.ascontiguousarray(np.broadcast_to(a[None], (128,) + a.shape))


def make_in_maps(inputs, ncores=8, nseq=2):
    f = lambda k: np.ascontiguousarray(np.asarray(inputs[k]))
    identf, U, negmask4, invf = _consts()
    shared = {
        "w_ada": f("w_ada"), "w_ffn_in": f("w_ffn_in"), "w_ffn_out": f("w_ffn_out"), "w_in": f("w_in"), "w_out": f("w_out"),
        "b_adaT": np.ascontiguousarray(f("b_ada").reshape(DEPTH, 72, 128).transpose(2, 0, 1)),
        "conv_wT": np.ascontiguousarray(f("conv_w").reshape(DEPTH, 4, 8, 128).transpose(3, 0, 1, 2)),
        "conv_bT": np.ascontiguousarray(f("conv_b").reshape(DEPTH, 8, 128).transpose(2, 0, 1)),
        "dtb": _rep(f("dt_bias")), "alog": _rep(f("a_log")), "dsk": _rep(f("d_skip")),
        "ssdnwT": np.ascontiguousarray(f("ssd_norm_w").reshape(DEPTH, 4, 128).transpose(2, 0, 1)),
        "sublnT": np.ascontiguousarray(f("subln_w").transpose(1, 0)),
        "dlam": _rep(f("diff_lambda")),
        "ln_g_rep": _rep(f("ln_g").reshape(-1)), "ln_b_rep": _rep(f("ln_b").reshape(-1)),
        "identf": identf, "Umat": U, "negmask4": negmask4, "invf": invf,
    }
    x = f("x"); c = f("c"); pos = f("positions")
    maps = []
    for ci in range(ncores):
        bs = [2 * ci + k for k in range(2)]
        m = dict(shared)
        m["x"] = np.ascontiguousarray(x[bs])
        m["cT"] = np.ascontiguousarray(c[bs].reshape(2, 8, 128).transpose(2, 1, 0))
        m["pos"] = np.ascontiguousarray(pos[bs].reshape(2, NT, 128).transpose(2, 0, 1)).astype(np.int32)
        maps.append(m)
    return maps


def kernel(**inputs):
    if "nc" not in _CACHE:
        _CACHE["nc"] = build()[0]
    nc = _CACHE["nc"]
    maps = make_in_maps(inputs)
    res = run_bass_kernel_spmd(nc, maps, core_ids=list(range(8)))
    out = np.concatenate([np.asarray(r["out"]) for r in res.results], axis=0)
    return out.astype(np.float32)
```

```python
import math
import types
import numpy as np
import concourse.bass as bass
import concourse.mybir as mybir
from concourse.bass_utils import run_bass_kernel_spmd

F32 = mybir.dt.float32
BF16 = mybir.dt.bfloat16
I32 = mybir.dt.int32
AF = mybir.ActivationFunctionType
ALU = mybir.AluOpType

D = 1024
T = 2048
NT = 16
TB = 512
NB = 4
DFF = 2816
NJ = 22
NJG = 11
DEPTH = 4
INC = 3080
ALPHA = (2 * DEPTH) ** 0.25
EPS = 1e-5
NEG = -30000.0
TWO_PI = 2.0 * math.pi


def _freeze(fn):
    if fn is None or fn.__closure__ is None:
        return fn
    cells = []
    for c in fn.__closure__:
        try:
            cells.append(types.CellType(c.cell_contents))
        except ValueError:
            cells.append(c)
    g = types.FunctionType(fn.__code__, fn.__globals__, fn.__name__, fn.__defaults__, tuple(cells))
    g.__kwdefaults__ = fn.__kwdefaults__
    return g


class Clock:
    __slots__ = ("sem", "count", "name")

    def __init__(self, sem, name):
        self.sem = sem
        self.count = 0
        self.name = name


class Tok:
    __slots__ = ("w", "r", "name")

    def __init__(self, name=""):
        self.w = None
        self.r = []
        self.name = name


class Prog:
    ENG = ("pe", "act", "dve", "pool", "sp")

    def __init__(self, nc):
        self.nc = nc
        self.clk = {k: Clock(nc.alloc_semaphore(name="c_" + k), k) for k in self.ENG}
        self.streams = {k: [] for k in self.ENG}
        self.known = {k: {} for k in self.ENG}
        self.ninstr = 0

    def new_clock(self, name):
        return Clock(self.nc.alloc_semaphore(name="d_" + name), name)

    def _waits(self, eng, reads, writes, same_ok):
        need = {}
        own = self.clk[eng]

        def req(cv):
            if cv is None:
                return
            c, v = cv
            if same_ok and c is own:
                return
            if need.get(c, 0) < v:
                need[c] = v

        for t in reads:
            req(t.w)
        for t in writes:
            req(t.w)
            for cv in t.r:
                req(cv)
        kn = self.known[eng]
        out = []
        for c, v in need.items():
            if kn.get(c, 0) >= v:
                continue
            kn[c] = v
            out.append((c.sem, v))
        return out

    def _mark(self, me, reads, writes):
        for t in reads:
            t.r.append(me)
            if len(t.r) > 64:
                best = {}
                for c, v in t.r:
                    if best.get(c, 0) < v:
                        best[c] = v
                t.r = list(best.items())
        for t in writes:
            t.w = me
            t.r = []

    def op(self, eng, fn, reads=(), writes=(), same_ok=False):
        waits = self._waits(eng, reads, writes, same_ok)
        c = self.clk[eng]
        c.count += 1
        me = (c, c.count)
        self.streams[eng].append((waits, _freeze(fn), c.sem, 1))
        self._mark(me, reads, writes)
        self.ninstr += 1
        return me

    def group(self, eng, fns, reads=(), writes=(), same_ok=False):
        waits = self._waits(eng, reads, writes, same_ok)
        c = self.clk[eng]
        c.count += 1
        me = (c, c.count)
        n = len(fns)
        for i, fn in enumerate(fns):
            self.streams[eng].append((waits if i == 0 else [], _freeze(fn), c.sem if i == n - 1 else None, 1))
        self._mark(me, reads, writes)
        self.ninstr += n
        return me

    def dma(self, q, dclk, pairs, reads=(), writes=(), **kw):
        waits = self._waits(q, reads, writes, False)
        for i, (o, s) in enumerate(pairs):
            dclk.count += 16
            self.streams[q].append((waits if i == 0 else [],
                                    (lambda e, o=o, s=s, k=kw: e.dma_start(out=o, in_=s, **k)),
                                    dclk.sem, 16))
        me = (dclk, dclk.count)
        self._mark(me, reads, writes)
        self.ninstr += len(pairs)
        return me

    def dma_throttled(self, q, clks, pairs, writes=()):
        n = len(clks)
        for i, (o, s) in enumerate(pairs):
            c = clks[i % n]
            w = [(c.sem, c.count)] if c.count > 0 else []
            c.count += 16
            self.streams[q].append((w, (lambda e, o=o, s=s: e.dma_start(out=o, in_=s)), c.sem, 16))
        for t, c in zip(writes, clks):
            t.w = (c, c.count)
            t.r = []
        self.ninstr += len(pairs)

    def barrier(self):
        self.last_barrier = [(self.clk[o].sem, self.clk[o].count) for o in ("pe", "act", "dve") if self.clk[o].count > 0]
        for e in ("pe", "act", "dve", "pool"):
            waits = []
            kn = self.known[e]
            for o in ("pe", "act", "dve", "pool"):
                if o == e:
                    continue
                c = self.clk[o]
                if c.count > kn.get(c, 0):
                    kn[c] = c.count
                    waits.append((c.sem, c.count))
            if waits:
                self.streams[e].append((waits, None, None, 0))

    def wait_barrier(self, q):
        if getattr(self, "last_barrier", None):
            self.streams[q].append((list(self.last_barrier), None, None, 0))

    def finish(self, toks):
        waits = self._waits("sp", [], toks, False)
        self.streams["sp"].append((waits, None, None, 0))

    def emit(self):
        nc = self.nc
        with nc.Block() as block:
            def run(name):
                def body(eng):
                    for waits, fn, sem, inc in self.streams[name]:
                        for s, v in waits:
                            eng.wait_ge(s, v)
                        if fn is None:
                            continue
                        ins = fn(eng)
                        if sem is not None:
                            ins.then_inc(sem, inc)
                return body
            block.tensor(run("pe"))
            block.scalar(run("act"))
            block.vector(run("dve"))
            block.gpsimd(run("pool"))
            block.sync(run("sp"))


class Slot:
    __slots__ = ("ap", "tok", "clk", "clk_sw")

    def __init__(self, ap, tok, clk=None, clk_sw=None):
        self.ap = ap
        self.tok = tok
        self.clk = clk
        self.clk_sw = clk_sw


NW = 2
ARENA = 22940
ATT_LA = 2


def build(nlayers=DEPTH, nseq=2, stop=None, dbg=None):
    nc = bass.Bass("TRN2", target_bir_lowering=False)
    P = Prog(nc)

    def din(name, shape, dt=F32):
        return nc.dram_tensor(name, list(shape), dt, kind="ExternalInput").ap()

    x_d = din("x", [2, T, D])
    cT_d = din("cT", [128, 8, 2])
    pos_d = din("pos", [128, 2, NT], I32)
    wada_d = din("w_ada", [DEPTH, D, 9 * D])
    badaT_d = din("b_adaT", [128, DEPTH, 72])
    wfi_d = din("w_ffn_in", [DEPTH, 2, D, 2 * DFF])
    wfo_d = din("w_ffn_out", [DEPTH, 2, DFF, D])
    win_d = din("w_in", [DEPTH, D, INC])
    wout_d = din("w_out", [DEPTH, D, D])
    convw_d = din("conv_wT", [128, DEPTH, 4, 8])
    convb_d = din("conv_bT", [128, DEPTH, 8])
    dtb_d = din("dtb", [128, DEPTH, 8])
    alog_d = din("alog", [128, DEPTH, 8])
    dsk_d = din("dsk", [128, DEPTH, 8])
    ssdnw_d = din("ssdnwT", [128, DEPTH, 4])
    subln_d = din("sublnT", [128, DEPTH])
    dlam_d = din("dlam", [128, DEPTH, 4, 64])
    lng_d = din("ln_g_rep", [128, DEPTH * 3 * D])
    lnb_d = din("ln_b_rep", [128, DEPTH * 3 * D])
    identf_d = din("identf", [128, 128])
    U_d = din("Umat", [128, 128])
    negm_d = din("negmask4", [128, 512])
    invf_d = din("invf", [128, 8])
    out_d = nc.dram_tensor("out", [2, T, D], F32, kind="ExternalOutput").ap()
    dbg_d = nc.dram_tensor("dbg", [128, 8192], F32, kind="ExternalOutput").ap() if dbg else None

    class StopBuild(Exception):
        pass

    def DBG(name, ap, toks):
        if dbg != name:
            return
        n = int(np.prod(ap.shape[1:]))
        src = ap
        if len(ap.shape) == 3:
            src = ap.rearrange("p a b -> p (a b)") if False else ap
        ck = P.new_clock("dbg")
        tk = Tok("dbg")
        dst = dbg_d[:, 0:n]
        if len(ap.shape) == 3:
            dst = dst.rearrange("p (a b) -> p a b", a=ap.shape[1])
        elif len(ap.shape) == 4:
            dst = dst.rearrange("p (a b c) -> p a b c", a=ap.shape[1], b=ap.shape[2])
        P.dma("pool", ck, [(dst, src)], reads=list(toks), writes=[tk])
        P.finish([tk])
        raise StopBuild()

    def dscr(name, shape):
        return nc.dram_tensor(name, list(shape), BF16, kind="Internal").ap()

    wbfi = dscr("wbfi", [DEPTH, 2, NJG, 128, 8, 2, 256])
    wbfo = dscr("wbfo", [DEPTH, 2, 2, 3, 128, 8, 512])
    wbmi = dscr("wbmi", [DEPTH, 6, 128, 8, 512])
    wbmd = dscr("wbmd", [DEPTH, 128, 8, 8])
    wbmo = dscr("wbmo", [DEPTH, 2, 128, 8, 512])

    def sb(name, shape, dt=F32):
        return nc.alloc_sbuf_tensor("s_" + name, list(shape), dt)

    X = sb("X", [128, NT, D])
    Xtok = [Tok(f"X{t}") for t in range(NT)]
    identf = sb("identf", [128, 128]); identb = sb("identb", [128, 128], BF16)
    Umat = sb("Umat", [128, 128]); negm = sb("negm", [128, 512]); onesf = sb("onesf", [128, 128])
    invf = sb("invf", [128, 8])
    cT = sb("cT", [128, 8, 2]); scT = sb("scT", [128, 8, 2], BF16)
    posi = sb("posi", [128, 2, NT], I32)
    badaT = sb("badaT", [128, DEPTH, 72])
    modT = sb("modT", [128, DEPTH, 72, 2])
    convw = sb("convw", [128, DEPTH, 4, 8]); convb = sb("convb", [128, DEPTH, 8])
    dtb = sb("dtb", [128, DEPTH, 8]); alog = sb("alog", [128, DEPTH, 8]); dsk = sb("dsk", [128, DEPTH, 8])
    Aneg = sb("Aneg", [128, DEPTH, 8])
    ssdnw = sb("ssdnw", [128, DEPTH, 4]); subln = sb("subln", [128, DEPTH]); sublnS = sb("sublnS", [128, DEPTH])
    lamt = sb("lamt", [128, DEPTH, 2]); nlam = sb("nlam", [128, DEPTH])
    cosT = sb("cosT", [128, NT, 8]); sinT = sb("sinT", [128, NT, 8])
    sc1 = sb("sc1", [128, 8]); gcol = sb("gcol", [128, 8]); epsT = sb("epsT", [128, 1])
    gate_b = sb("gate_b", [128, D]); lng_b = sb("lng_b", [128, D]); lnb_b = sb("lnb_b", [128, D])
    st6 = sb("st6", [128, 4, 2, 6]); mv = sb("mv", [128, 4, 2]); rstd = sb("rstd", [128, 4]); nmr = sb("nmr", [128, 4])
    est6 = sb("est6", [128, 2, 6]); emv = sb("emv", [128, 2]); erstd = sb("erstd", [128, 1]); enmr = sb("enmr", [128, 1])
    Wring = [sb(f"Wring{i}", [128, 8, 512], BF16) for i in range(NW)]
    uT0 = sb("uT0", [128, 8, TB], BF16)
    tmp5 = [sb(f"tmp5_{i}", [128, 512]) for i in range(3)]
    dg = [sb(f"dg{i}", [128, 128]) for i in range(2)]

    ARENA_F32 = ARENA
    arena = sb("arena", [128, ARENA_F32])

    class Carver:
        def __init__(self):
            self.off = 0

        def take(self, shape, dt=F32):
            n = int(np.prod(shape[1:]))
            nf = n if dt == F32 else (n + 1) // 2
            a = arena[:, self.off:self.off + nf]
            self.off += nf
            assert self.off <= ARENA_F32, self.off
            if dt != F32:
                a = a.bitcast(dt)[:, 0:n]
            if len(shape) == 3:
                a = a.rearrange("p (a b) -> p a b", a=shape[1])
            elif len(shape) == 4:
                a = a.rearrange("p (a b c) -> p a b c", a=shape[1], b=shape[2])
            return a

    cf = Carver()
    actT = cf.take([128, NJ, TB], BF16)
    uT1 = cf.take([128, 8, TB], BF16)
    xhat_f = [cf.take([128, D]) for _ in range(2)]
    dlam = cf.take([128, DEPTH, 4, 64])
    WF_extra = [cf.take([128, 8, 512], BF16) for _ in range(2)]
    uTb = [uT0[:], uT1]
    cm = Carver()
    KT = cm.take([128, 4, T], BF16)
    Vc = cm.take([128, NT, 4, 130], BF16)
    QT2 = [cm.take([128, 4, 128], BF16) for _ in range(2)]
    stag = [cm.take([128, 516])] * 2
    hist = cm.take([128, 8, 4])
    BCT = cm.take([128, 4, TB], BF16)
    Btm = cm.take([128, 4, 256], BF16)
    xs_tm = cm.take([128, 4, 512])
    xdt = cm.take([128, 4, 512], BF16)
    xdt2 = cm.take([128, 512], BF16)
    aU2 = [cm.take([128, 4, 128]) for _ in range(2)]
    Eb2 = [cm.take([128, 4, 128], BF16) for _ in range(2)]
    Mb = cm.take([128, 8, 128], BF16)
    PTr = [cm.take([128, 2, 4, 128], BF16) for _ in range(3)]
    yoT = cm.take([128, 8, TB], BF16)
    prev = cm.take([128, 512]); prevb = cm.take([128, 512], BF16)
    yo2 = cm.take([128, 1024])
    ytmp = yo2[:, 0:512]; otmp = yo2[:, 512:1024]; xhat_m = yo2
    qk_tm = [ytmp, cm.take([128, 512])]
    zs = cm.take([128, 512])
    ropet = zs[:, 0:256].rearrange("p (a b c) -> p a b c", a=8, b=4)
    dts = cm.take([128, 4, 8]); a_t = cm.take([128, 4, 8]); acs = cm.take([128, 4, 8]); nacs = cm.take([128, 4, 8])
    ea = cm.take([128, 4, 8]); cd = cm.take([128, 4, 8]); dte = cm.take([128, 4, 8]); dtd = cm.take([128, 4, 8])
    ss = cm.take([128, 8]); rr = cm.take([128, 8])
    print("arena floats: ffn", cf.off, "mixer", cm.off, "sbuf left", nc.sbuf_bytes_remaining)

    banks = [nc.alloc_psum_tensor(f"bank{i}", [128, 512], F32) for i in range(8)]
    btok = [Tok(f"bank{i}") for i in range(8)]

    def tok(n=""):
        return Tok(n)

    t_const = tok("const")
    t_modl = [tok(f"mod{l}") for l in range(DEPTH)]
    t_rows = {"gate": tok(), "lng": tok(), "lnb": tok()}
    t_sc = tok("sc1gcol")
    t_stat = tok("stat")
    t_estat = tok("estat")
    t_xhat_f = [tok("xhat0"), tok("xhat1")]
    t_tmp5 = [tok() for _ in range(3)]
    t_dg = [tok(), tok()]
    t_uT = [tok(), tok()]
    t_act = tok("actT")
    Wslots = [Slot(Wring[i], tok(f"W{i}"), P.new_clock(f"W{i}"), P.new_clock(f"Ws{i}")) for i in range(NW)]
    wctr = [0]

    Wslots_f = Wslots + [Slot(WF_extra[i], tok(f"WF{i}"), P.new_clock(f"WF{i}"), P.new_clock(f"WFs{i}")) for i in range(2)]
    wfctr = [0]

    arena_gate = {2: None, 3: None}

    def next_w(ffn=False):
        if ffn:
            idx = wfctr[0] % len(Wslots_f)
            s = Wslots_f[idx]
            wfctr[0] += 1
            if idx >= 2:
                lb = getattr(P, "last_barrier", None)
                if lb is not None and arena_gate[idx] is not lb:
                    arena_gate[idx] = lb
                    P.wait_barrier("sp")
            return s
        s = Wslots[wctr[0] % NW]
        wctr[0] += 1
        return s

    tm = {n: tok(n) for n in ("KT", "V", "QT", "hist", "BCT", "Btm", "xs", "xdt", "xdt2", "aU0", "aU1", "E0", "E1", "M", "yoT",
                               "prev", "prevb", "ytmp", "otmp", "zs", "dt", "ss", "stag0", "stag1",
                               "qk1", "pt0", "pt1", "pt2")}
    tm["qk0"] = tm["ytmp"]
    tm["QT0"] = tok("QT0"); tm["QT1"] = tok("QT1")
    tm["stag1"] = tm["stag0"]
    tm["rope"] = tm["zs"]

    c_const = P.new_clock("const")
    c_rows = P.new_clock("rows")
    c_x = [P.new_clock(f"x{b}") for b in range(NB)]
    c_out = P.new_clock("out")
    t_out = tok("out")

    small = [(identf, identf_d), (Umat, U_d), (negm, negm_d), (invf, invf_d), (cT, cT_d), (posi, pos_d),
             (badaT, badaT_d), (convw, convw_d), (convb, convb_d), (dtb, dtb_d), (alog, alog_d), (dsk, dsk_d),
             (ssdnw, ssdnw_d), (subln, subln_d), (dlam, dlam_d)]
    P.dma("sp", c_const, [((a if a is dlam else a[:]), b) for a, b in small], writes=[t_const])
    V = lambda f, **k: P.op("dve", f, **k)
    A = lambda f, **k: P.op("act", f, **k)
    G = lambda f, **k: P.op("dve", f, **k)
    V(lambda e: e.memset(onesf[:], 1.0), writes=[t_const])
    V(lambda e: e.memset(epsT[:], EPS), writes=[t_const])
    V(lambda e: e.tensor_copy(out=identb[:], in_=identf[:]), reads=[t_const], writes=[t_const])
    A(lambda e: e.activation(out=scT[:], in_=cT[:], func=AF.Silu), reads=[t_const], writes=[t_const])
    A(lambda e: e.activation(out=Aneg[:], in_=alog[:], func=AF.Exp), reads=[t_const], writes=[t_const])
    V(lambda e: e.tensor_scalar(out=Aneg[:], in0=Aneg[:], scalar1=-1.0, scalar2=None, op0=ALU.mult), reads=[t_const], writes=[t_const])
    for l in range(DEPTH):
        for i in range(2):
            V(lambda e, l=l, i=i: e.tensor_tensor(out=tmp5[0][:, 0:64], in0=dlam[:, l, 2 * i, :], in1=dlam[:, l, 2 * i + 1, :], op=ALU.mult),
              reads=[t_const], writes=[t_tmp5[0]])
            V(lambda e, l=l, i=i: e.tensor_reduce(out=lamt[:, l, i:i + 1], in_=tmp5[0][:, 0:64], axis=mybir.AxisListType.X, op=ALU.add),
              reads=[t_tmp5[0]], writes=[t_const])
    A(lambda e: e.activation(out=lamt[:], in_=lamt[:], func=AF.Exp), reads=[t_const], writes=[t_const])
    for l in range(DEPTH):
        lam_init = 0.8 - 0.6 * math.exp(-0.3 * l)
        V(lambda e, l=l, li=lam_init: e.scalar_tensor_tensor(out=nlam[:, l:l + 1], in0=lamt[:, l, 0:1], scalar=-1.0, in1=lamt[:, l, 1:2], op0=ALU.mult, op1=ALU.add),
          reads=[t_const], writes=[t_const])
        V(lambda e, l=l, li=lam_init: e.tensor_scalar(out=nlam[:, l:l + 1], in0=nlam[:, l:l + 1], scalar1=-li, scalar2=None, op0=ALU.add),
          reads=[t_const], writes=[t_const])
        V(lambda e, l=l, li=lam_init: e.tensor_scalar(out=sublnS[:, l:l + 1], in0=subln[:, l:l + 1], scalar1=(1.0 - li), scalar2=None, op0=ALU.mult),
          reads=[t_const], writes=[t_const])

    wada_v = [wada_d[l].rearrange("(kc p) c -> p kc c", p=128) for l in range(DEPTH)]

    def mod_slices(l, sl0, sl1, ffn_ring):
        for sl in range(sl0, sl1):
            s_ = next_w(ffn=ffn_ring)
            P.dma("pool", s_.clk_sw, [(s_.ap[:], wada_v[l][:, :, sl * 512:(sl + 1) * 512])], writes=[s_.tok])
            fns = []
            for mc in range(4):
                for kc in range(8):
                    fns.append(lambda e, s_=s_, mc=mc, kc=kc: e.matmul(
                        out=banks[0][:, 2 * mc:2 * mc + 2], lhsT=s_.ap[:, kc, mc * 128:(mc + 1) * 128], rhs=scT[:, kc, :],
                        start=(kc == 0), stop=(kc == 7)))
            P.group("pe", fns, reads=[s_.tok, t_const], writes=[btok[0]], same_ok=True)
            V(lambda e, l=l, sl=sl: e.tensor_tensor(out=modT[:, l, 4 * sl:4 * sl + 4, :], in0=banks[0][:, 0:8].rearrange("p (m s) -> p m s", s=2),
                                                    in1=badaT[:, l, 4 * sl:4 * sl + 4].unsqueeze(2).broadcast_to([128, 4, 2]), op=ALU.add),
              reads=[btok[0], t_const], writes=[t_modl[l]])

    mod_slices(0, 0, 18, False)

    t_wbf = {}

    pc_clks = [P.new_clock(f"pc{i}") for i in range(8)]

    def precast(l):
        for f in range(2):
            tk = [tok(f"wbf{l}{f}_{i}") for i in range(8)]
            t_wbf[(l, f)] = tk
            src = wfi_d[l, f].rearrange("(kc p) c -> p kc c", p=128)
            pairs = []
            for jg in range(NJG):
                for gu in range(2):
                    pairs.append((wbfi[l, f, jg, :, :, gu, :], src[:, :, gu * DFF + jg * 256: gu * DFF + (jg + 1) * 256]))
            src2 = wfo_d[l, f].rearrange("(j p) c -> p j c", p=128)
            for dmh in range(2):
                for fl in range(3):
                    nj = 8 if fl < 2 else 6
                    pairs.append((wbfo[l, f, dmh, fl, :, 0:nj, :], src2[:, 8 * fl:8 * fl + nj, dmh * 512:(dmh + 1) * 512]))
            if f == 0:
                P.dma_throttled("pool", pc_clks, pairs, writes=tk)
                tkm = [tok(f"wbm{l}_{i}") for i in range(8)]
                t_wbf[(l, "m")] = tkm
                srcm = win_d[l].rearrange("(kc p) c -> p kc c", p=128)
                grp_cols = [0, 512, 1024, 1544, 2056, 2568]
                pm = [(wbmi[l, g], srcm[:, :, c0:c0 + 512]) for g, c0 in enumerate(grp_cols)]
                pm.append((wbmd[l], srcm[:, :, 1536:1544]))
                srco = wout_d[l].rearrange("(kc p) c -> p kc c", p=128)
                for dmh in range(2):
                    pm.append((wbmo[l, dmh], srco[:, :, dmh * 512:(dmh + 1) * 512]))
                P.dma_throttled("pool", pc_clks, pm, writes=tkm)
            else:
                P.dma_throttled("pool", pc_clks, pairs, writes=tk)


    precast(0)

    def load_rows(l, sub):
        o = (l * 3 + sub) * D
        P.dma("sp", c_rows, [(lng_b[:], lng_d[:, o:o + D]), (lnb_b[:], lnb_d[:, o:o + D])],
              writes=[t_rows["lng"], t_rows["lnb"]])

    def sub_prep(l, sub, s, gate_mul):
        m0 = sub * 24
        V(lambda e: e.tensor_scalar(out=sc1[:], in0=modT[:, l, m0 + 8:m0 + 16, s], scalar1=1.0, scalar2=None, op0=ALU.add),
          reads=[t_modl[l]], writes=[t_sc])
        V(lambda e: e.tensor_scalar(out=gcol[:], in0=modT[:, l, m0 + 16:m0 + 24, s], scalar1=1.0, scalar2=gate_mul, op0=ALU.add, op1=ALU.mult),
          reads=[t_modl[l]], writes=[t_sc])
        for half in range(2):
            bk = 4 + half
            for q in range(4):
                kc = half * 4 + q
                i = kc % 2
                V(lambda e, kc=kc, i=i: e.tensor_scalar(out=dg[i][:], in0=identf[:], scalar1=gcol[:, kc:kc + 1], scalar2=None, op0=ALU.mult),
                  reads=[t_sc, t_const], writes=[t_dg[i]])
                P.group("pe", [lambda e, q=q, i=i, bk=bk: e.matmul(out=banks[bk][:, q * 128:(q + 1) * 128], lhsT=onesf[:], rhs=dg[i][:], start=True, stop=True)],
                        reads=[t_dg[i], t_const], writes=[btok[bk]], same_ok=True)
            A(lambda e, half=half, bk=bk: e.activation(out=gate_b[:, half * 512:(half + 1) * 512], in_=banks[bk][:], func=AF.Copy),
              reads=[btok[bk]], writes=[t_rows["gate"]])

    def phase_a_parts(l, sub, s, b, ub, xhats):
        m0 = sub * 24
        tiles = [4 * b + i for i in range(4)]

        def stats():
            for i, tt in enumerate(tiles):
                for hf in range(2):
                    V(lambda e, i=i, tt=tt, hf=hf: e.bn_stats(out=st6[:, i, hf, :], in_=X[:, tt, hf * 512:(hf + 1) * 512]),
                      reads=[Xtok[tt]], writes=[t_stat])
                V(lambda e, i=i: e.bn_aggr(out=mv[:, i, :], in_=st6[:, i, :, :].rearrange("p a b -> p (a b)")), reads=[t_stat], writes=[t_stat])
            A(lambda e: e.activation(out=rstd[:], in_=mv[:, :, 1], func=AF.Ln, bias=epsT[:, 0:1]), reads=[t_stat, t_const], writes=[t_stat])
            A(lambda e: e.activation(out=rstd[:], in_=rstd[:], func=AF.Exp, scale=-0.5), reads=[t_stat], writes=[t_stat])
            V(lambda e: e.scalar_tensor_tensor(out=nmr[:], in0=mv[:, :, 0], scalar=-1.0, in1=rstd[:], op0=ALU.mult, op1=ALU.mult),
              reads=[t_stat], writes=[t_stat])

        def mk_xhat(i):
            tt = tiles[i]
            xhat, xh_toks = xhats[i % len(xhats)]

            def f():
                A(lambda e: e.activation(out=xhat, in_=X[:, tt, :], func=AF.Identity, scale=rstd[:, i:i + 1], bias=nmr[:, i:i + 1]),
                  reads=[Xtok[tt], t_stat], writes=xh_toks)
            return f

        def mk_T(i):
            xhat, xh_toks = xhats[i % len(xhats)]
            b0 = (i % 2) * 2

            def f():
                for hb in range(2):
                    P.group("pe", [lambda e, q=q, hb=hb: e.transpose(out=banks[b0 + hb][:, q * 128:(q + 1) * 128],
                                                                     in_=xhat[:, (hb * 4 + q) * 128:(hb * 4 + q + 1) * 128], identity=identf[:])
                                   for q in range(4)],
                            reads=xh_toks + [t_const], writes=[btok[b0 + hb]], same_ok=True)
                for kc in range(8):
                    bk = b0 + kc // 4
                    q = kc % 4
                    if kc % 2 == 0:
                        A(lambda e, kc=kc, bk=bk, q=q: e.activation(out=uTb[ub][:, kc, i * 128:(i + 1) * 128], in_=banks[bk][:, q * 128:(q + 1) * 128],
                                                                     func=AF.Identity, scale=sc1[:, kc:kc + 1], bias=modT[:, l, m0 + kc:m0 + kc + 1, s]),
                          reads=[btok[bk], t_sc, t_modl[l]], writes=[t_uT[ub]])
                    else:
                        V(lambda e, kc=kc, bk=bk, q=q: e.tensor_scalar(out=uTb[ub][:, kc, i * 128:(i + 1) * 128], in0=banks[bk][:, q * 128:(q + 1) * 128],
                                                                        scalar1=sc1[:, kc:kc + 1], scalar2=modT[:, l, m0 + kc:m0 + kc + 1, s],
                                                                        op0=ALU.mult, op1=ALU.add),
                          reads=[btok[bk], t_sc, t_modl[l]], writes=[t_uT[ub]])
            return f

        return stats, [mk_xhat(i) for i in range(4)], [mk_T(i) for i in range(4)]

    def phase_a(l, sub, s, b, ub, xhats):
        st, xs_, ts_ = phase_a_parts(l, sub, s, b, ub, xhats)
        st()
        for i in range(4):
            xs_[i]()
            ts_[i]()

    def epilogue_parts(tt, bk, half, last):
        k = 2
        xs_ = X[:, tt, half * 512:(half + 1) * 512]

        def p1():
            V(lambda e: e.tensor_tensor(out=tmp5[k][:], in0=banks[bk][:], in1=gate_b[:, half * 512:(half + 1) * 512], op=ALU.mult),
              reads=[btok[bk], t_rows["gate"]], writes=[t_tmp5[k]])
            V(lambda e: e.scalar_tensor_tensor(out=xs_, in0=xs_, scalar=ALPHA, in1=tmp5[k][:], op0=ALU.mult, op1=ALU.add),
              reads=[t_tmp5[k], Xtok[tt]], writes=[Xtok[tt]])

        def p2():
            for hf in range(2):
                V(lambda e, hf=hf: e.bn_stats(out=est6[:, hf, :], in_=X[:, tt, hf * 512:(hf + 1) * 512]), reads=[Xtok[tt]], writes=[t_estat])
            V(lambda e: e.bn_aggr(out=emv[:], in_=est6[:].rearrange("p a b -> p (a b)")), reads=[t_estat], writes=[t_estat])
            A(lambda e: e.activation(out=erstd[:], in_=emv[:, 1:2], func=AF.Ln, bias=epsT[:, 0:1]), reads=[t_estat, t_const], writes=[t_estat])
            A(lambda e: e.activation(out=erstd[:], in_=erstd[:], func=AF.Exp, scale=-0.5), reads=[t_estat], writes=[t_estat])

        def p3():
            V(lambda e: e.scalar_tensor_tensor(out=enmr[:], in0=emv[:, 0:1], scalar=-1.0, in1=erstd[:], op0=ALU.mult, op1=ALU.mult),
              reads=[t_estat], writes=[t_estat])
            A(lambda e: e.activation(out=X[:, tt, :], in_=X[:, tt, :], func=AF.Identity, scale=erstd[:, 0:1], bias=enmr[:, 0:1]),
              reads=[t_estat, Xtok[tt]], writes=[Xtok[tt]])
            V(lambda e: e.tensor_tensor(out=X[:, tt, :], in0=X[:, tt, :], in1=lng_b[:], op=ALU.mult), reads=[Xtok[tt], t_rows["lng"]], writes=[Xtok[tt]])

        def p4():
            V(lambda e: e.tensor_tensor(out=X[:, tt, :], in0=X[:, tt, :], in1=lnb_b[:], op=ALU.add), reads=[Xtok[tt], t_rows["lnb"]], writes=[Xtok[tt]])

        return [p1, p2, p3, p4] if last else [p1]

    def epilogue(tt, bk, half, last):
        for p in epilogue_parts(tt, bk, half, last):
            p()

    epi_pending = []

    def ffn_block(l, f, s, b, ub, hook=None, pre=None):
        tkw = t_wbf[(l, f)]
        if pre is not None:
            pre()
        for jg in range(NJG):
            s_ = next_w(ffn=True)
            P.dma("sp", s_.clk, [(s_.ap[:].rearrange("p k c -> p (k c)"), wbfi[l, f, jg].rearrange("p k g c -> p (k g c)"))],
                  reads=list(tkw), writes=[s_.tok])
            if b == 0 and jg == 2:
                load_rows(l, 0 if f == 0 else 2)
            if jg == 1 and hook is not None:
                for h_ in hook.get("B", ()):
                    h_()
            wflat = s_.ap[:].rearrange("p k c -> p (k c)").rearrange("p (k g c) -> p k g c", k=8, g=2)
            DBG("w0", s_.ap[:], [s_.tok])
            DBG("uT", uTb[ub], [t_uT[ub]])
            for jj in range(2):
                j = 2 * jg + jj
                bg = (j % 2) * 2
                for gu in range(2):
                    P.group("pe", [lambda e, kc=kc, gu=gu, jj=jj, bg=bg: e.matmul(
                        out=banks[bg + gu][:], lhsT=wflat[:, kc, gu, jj * 128:(jj + 1) * 128], rhs=uTb[ub][:, kc, :],
                        start=(kc == 0), stop=(kc == 7)) for kc in range(8)],
                        reads=[s_.tok, t_uT[ub]], writes=[btok[bg + gu]], same_ok=True)
                k = j % 2
                A(lambda e, bg=bg, k=k: e.activation(out=tmp5[k][:], in_=banks[bg][:], func=AF.Silu), reads=[btok[bg]], writes=[t_tmp5[k]])
                V(lambda e, bg=bg, k=k, j=j: e.tensor_tensor(out=actT[:, j, :], in0=banks[bg + 1][:], in1=tmp5[k][:], op=ALU.mult),
                  reads=[btok[bg + 1], t_tmp5[k]], writes=[t_act])
                if epi_pending and j >= 1:
                    epi_pending.pop(0)()
        DBG("actT", actT[:, 0:16, :], [t_act])
        for dmh in range(2):
            for fl in range(3):
                nj = 8 if fl < 2 else 6
                so = next_w(ffn=True)
                P.dma("sp", so.clk, [(so.ap[:, 0:nj, :], wbfo[l, f, dmh, fl, :, 0:nj, :])], reads=list(tkw), writes=[so.tok])
                fns = []
                for jl in range(nj):
                    j = 8 * fl + jl
                    for i in range(4):
                        fns.append(lambda e, i=i, j=j, jl=jl, so=so: e.matmul(
                            out=banks[4 + i][:], lhsT=actT[:, j, i * 128:(i + 1) * 128], rhs=so.ap[:, jl, :],
                            start=(j == 0), stop=(j == NJ - 1)))
                P.group("pe", fns, reads=[so.tok, t_act], writes=[btok[4 + i] for i in range(4)], same_ok=True)
                if hook is not None:
                    for h_ in hook.get(dmh * 3 + fl, ()):
                        h_()
            for i in range(4):
                if dmh == 0:
                    epilogue(4 * b + i, 4 + i, dmh, False)
                else:
                    epi_pending.extend(epilogue_parts(4 * b + i, 4 + i, 1, True))

    def ffn_sublayer(l, f, s):
        sub = 0 if f == 0 else 2
        if f == 0 and s == 0 and l + 1 < nlayers:
            precast(l + 1)
        sub_prep(l, sub, s, 0.5)
        xh = [(xhat_f[0], [t_xhat_f[0]]), (xhat_f[1], [t_xhat_f[1]])]
        phase_a(l, sub, s, 0, 0, xh)
        for b in range(NB):
            nxt = None
            if b + 1 < NB:
                st, xs_, ts_ = phase_a_parts(l, sub, s, b + 1, (b + 1) % 2, xh)
                nxt = {0: [st, xs_[0], xs_[1]], 1: [ts_[0], xs_[2]], 2: [ts_[1], xs_[3]], 3: [ts_[2]], 4: [ts_[3]]}
            pre = None
            if f == 1 and s == 0 and l + 1 < nlayers:
                pre = (lambda b=b: mod_slices(l + 1, 5 * b, min(18, 5 * b + 5), True))
            ffn_block(l, f, s, b, b % 2, hook=nxt, pre=pre)
        while epi_pending:
            epi_pending.pop(0)()

    def rope_tables(s):
        posf = ytmp[:, 0:NT]
        ang = otmp[:, 0:NT * 8].rearrange("p (t i) -> p t i", i=8)
        kf = zs[:, 0:NT * 8].rearrange("p (t i) -> p t i", i=8)
        ki = xdt2[:, 0:2 * NT * 8].bitcast(I32).rearrange("p (t i) -> p t i", i=8)
        wr = qk_tm[1][:, 0:NT * 8].rearrange("p (t i) -> p t i", i=8)
        tk = tok("rope")
        V(lambda e: e.tensor_copy(out=posf, in_=posi[:, s, :]), reads=[t_const], writes=[tk])
        V(lambda e: e.tensor_tensor(out=ang, in0=posf.unsqueeze(2).broadcast_to([128, NT, 8]),
                                    in1=invf[:].unsqueeze(1).broadcast_to([128, NT, 8]), op=ALU.mult), reads=[tk, t_const], writes=[tk])
        V(lambda e: e.tensor_scalar(out=ki, in0=ang, scalar1=1.0 / TWO_PI, scalar2=None, op0=ALU.mult), reads=[tk], writes=[tk])
        V(lambda e: e.tensor_copy(out=kf, in_=ki), reads=[tk], writes=[tk])
        C1 = 6.28125
        C2 = TWO_PI - C1
        V(lambda e: e.scalar_tensor_tensor(out=ang, in0=kf, scalar=-C1, in1=ang, op0=ALU.mult, op1=ALU.add), reads=[tk], writes=[tk])
        V(lambda e: e.scalar_tensor_tensor(out=ang, in0=kf, scalar=-C2, in1=ang, op0=ALU.mult, op1=ALU.add), reads=[tk], writes=[tk])
        for dst, shift in ((sinT, 0.0), (cosT, math.pi / 2)):
            V(lambda e, dst=dst, shift=shift: e.tensor_scalar(out=dst[:], in0=ang, scalar1=float(shift), scalar2=None, op0=ALU.add), reads=[tk], writes=[tk])
            V(lambda e, dst=dst: e.tensor_scalar(out=wr, in0=dst[:], scalar1=math.pi, scalar2=-TWO_PI, op0=ALU.is_gt, op1=ALU.mult), reads=[tk], writes=[tk])
            V(lambda e, dst=dst: e.tensor_tensor(out=dst[:], in0=dst[:], in1=wr, op=ALU.add), reads=[tk], writes=[tk])
            V(lambda e, dst=dst: e.tensor_scalar(out=wr, in0=dst[:], scalar1=-math.pi, scalar2=TWO_PI, op0=ALU.is_lt, op1=ALU.mult), reads=[tk], writes=[tk])
            V(lambda e, dst=dst: e.tensor_tensor(out=dst[:], in0=dst[:], in1=wr, op=ALU.add), reads=[tk], writes=[tk])
            A(lambda e, dst=dst: e.activation(out=dst[:], in_=dst[:], func=AF.Sin), reads=[tk], writes=[tk])
        return tk

    rope_tok = [None]

    def mixer_sublayer(l, s):
        sub_prep(l, 1, s, 1.0)
        tkw = t_wbf[(l, "m")]
        if l == 0:
            rope_tok[0] = rope_tables(s)
        t_rope = rope_tok[0]
        V(lambda e: e.memset(hist, 0.0), writes=[tm["hist"]])
        V(lambda e: e.memset(prev, 0.0), writes=[tm["prev"]])
        V(lambda e: e.memset(prevb, 0.0), writes=[tm["prevb"]])
        V(lambda e: e.memset(Vc[:, :, :, 128:130], 1.0), writes=[tm["V"]])
        ptc = [0]
        hv = lambda ap: ap.rearrange("p (h q) -> p h q", q=64)
        for b in range(NB):
            ub = 0
            if b == 0:
                phase_a(l, 1, s, b, ub, [(xhat_m, [tm["ytmp"], tm["otmp"]])])
            uT = uTb[ub]
            sd = next_w()
            P.dma("sp", sd.clk, [(sd.ap[:, :, 0:8], wbmd[l])], reads=list(tkw), writes=[sd.tok])
            for i in range(4):
                P.group("pe", [lambda e, kc=kc, i=i, sd=sd: e.matmul(out=banks[7][:, i * 8:(i + 1) * 8], lhsT=uT[:, kc, i * 128:(i + 1) * 128],
                                                                      rhs=sd.ap[:, kc, 0:8], start=(kc == 0), stop=(kc == 7)) for kc in range(8)],
                        reads=[sd.tok, t_uT[ub]], writes=[btok[7]], same_ok=True)
            V(lambda e: e.tensor_tensor(out=dts, in0=banks[7][:, 0:32].rearrange("p (a h) -> p a h", h=8),
                                        in1=dtb[:, l, :].unsqueeze(1).broadcast_to([128, 4, 8]), op=ALU.add),
              reads=[btok[7], t_const], writes=[tm["dt"]])
            A(lambda e: e.activation(out=dts, in_=dts, func=AF.Exp), reads=[tm["dt"]], writes=[tm["dt"]])
            A(lambda e: e.activation(out=dts, in_=dts, func=AF.Ln, bias=1.0), reads=[tm["dt"]], writes=[tm["dt"]])
            V(lambda e: e.tensor_tensor(out=a_t, in0=dts, in1=Aneg[:, l, :].unsqueeze(1).broadcast_to([128, 4, 8]), op=ALU.mult),
              reads=[tm["dt"], t_const], writes=[tm["dt"]])
            a_flat = a_t.rearrange("p a h -> p (a h)")
            P.group("pe", [lambda e: e.matmul(out=banks[7][:, 64:96], lhsT=Umat[:], rhs=a_flat, start=True, stop=True),
                           lambda e: e.matmul(out=banks[7][:, 128:160], lhsT=onesf[:], rhs=a_flat, start=True, stop=True)],
                    reads=[tm["dt"], t_const], writes=[btok[7]], same_ok=True)
            acs_p = banks[7][:, 64:96].rearrange("p (a h) -> p a h", h=8)
            tot_p = banks[7][:, 128:160].rearrange("p (a h) -> p a h", h=8)
            V(lambda e: e.tensor_copy(out=acs, in_=acs_p), reads=[btok[7]], writes=[tm["dt"]])
            V(lambda e: e.tensor_scalar(out=nacs, in0=acs_p, scalar1=-1.0, scalar2=None, op0=ALU.mult), reads=[btok[7]], writes=[tm["dt"]])
            V(lambda e: e.tensor_tensor(out=dte, in0=tot_p, in1=acs, op=ALU.subtract), reads=[btok[7], tm["dt"]], writes=[tm["dt"]])
            A(lambda e: e.activation(out=dte, in_=dte, func=AF.Exp), reads=[tm["dt"]], writes=[tm["dt"]])
            A(lambda e: e.activation(out=ea, in_=acs, func=AF.Exp), reads=[tm["dt"]], writes=[tm["dt"]])
            A(lambda e: e.activation(out=cd, in_=tot_p, func=AF.Exp), reads=[btok[7]], writes=[tm["dt"]])
            V(lambda e: e.tensor_tensor(out=dtd, in0=dts, in1=dte, op=ALU.mult), reads=[tm["dt"]], writes=[tm["dt"]])

            conv_pending = []
            for grp in range(2):
                s_ = next_w()
                P.dma("sp", s_.clk, [(s_.ap[:], wbmi[l, 1 + grp])], reads=list(tkw), writes=[s_.tok])
                for c4 in range(4):
                    cc = grp * 4 + c4
                    bk = 4 + cc % 2
                    P.group("pe", [lambda e, kc=kc, c4=c4, bk=bk, s_=s_: e.matmul(out=banks[bk][:], lhsT=s_.ap[:, kc, c4 * 128:(c4 + 1) * 128],
                                                                               rhs=uT[:, kc, :], start=(kc == 0), stop=(kc == 7)) for kc in range(8)],
                            reads=[s_.tok, t_uT[ub]], writes=[btok[bk]], same_ok=True)
                    if len(conv_pending) > 0:
                        conv_pending.pop(0)()
                    sg = stag[cc % 2]
                    tsg = tm[f"stag{cc % 2}"]
                    A(lambda e, sg=sg, bk=bk: e.activation(out=sg[:, 3:515], in_=banks[bk][:], func=AF.Copy), reads=[btok[bk]], writes=[tsg])
                    V(lambda e, sg=sg, cc=cc: e.tensor_copy(out=sg[:, 0:3], in_=hist[:, cc, 0:3]), reads=[tm["hist"]], writes=[tsg])
                    V(lambda e, sg=sg, cc=cc: e.tensor_copy(out=hist[:, cc, 0:3], in_=sg[:, 512:515]), reads=[tsg], writes=[tm["hist"]])
                    k = cc % 2
                    co = tmp5[k]
                    tco = t_tmp5[k]
                    A(lambda e, sg=sg, cc=cc, co=co: e.activation(out=co[:], in_=sg[:, 0:512], func=AF.Identity, scale=convw[:, l, 0, cc:cc + 1],
                                                                  bias=convb[:, l, cc:cc + 1]), reads=[tsg, t_const], writes=[tco])
                    for tap in range(1, 4):
                        V(lambda e, sg=sg, cc=cc, co=co, tap=tap: e.scalar_tensor_tensor(out=co[:], in0=sg[:, tap:tap + 512], scalar=convw[:, l, tap, cc:cc + 1],
                                                                                         in1=co[:], op0=ALU.mult, op1=ALU.add),
                          reads=[tsg, t_const, tco], writes=[tco])
                    if cc < 4:
                        A(lambda e, co=co: e.activation(out=co[:], in_=co[:], func=AF.Silu), reads=[tco], writes=[tco])

                        def fin(cc=cc, co=co, tco=tco):
                            P.group("pe", [lambda e, i=i, co=co: e.transpose(out=banks[6][:, i * 128:(i + 1) * 128], in_=co[:, i * 128:(i + 1) * 128], identity=identf[:])
                                           for i in range(4)], reads=[tco, t_const], writes=[btok[6]], same_ok=True)
                            V(lambda e, cc=cc: e.tensor_copy(out=xs_tm[:, :, cc * 128:(cc + 1) * 128], in_=banks[6][:].rearrange("p (a c) -> p a c", c=128)),
                              reads=[btok[6]], writes=[tm["xs"]])
                        conv_pending.append(fin)
                    else:
                        A(lambda e, co=co, cc=cc: e.activation(out=BCT[:, cc - 4, :], in_=co[:], func=AF.Silu), reads=[tco], writes=[tm["BCT"]])
                        if cc < 6:
                            def fin(cc=cc):
                                bt = banks[6][:].bitcast(BF16)
                                P.group("pe", [lambda e, i=i, cc=cc, bt=bt: e.transpose(out=bt[:, i * 128:(i + 1) * 128], in_=BCT[:, cc - 4, i * 128:(i + 1) * 128], identity=identb[:])
                                               for i in range(4)], reads=[tm["BCT"], t_const], writes=[btok[6]], same_ok=True)
                                V(lambda e, cc=cc, bt=bt: e.tensor_copy(out=Btm[:, :, (cc - 4) * 128:(cc - 3) * 128], in_=bt[:, 0:512].rearrange("p (a c) -> p a c", c=128)),
                                  reads=[btok[6]], writes=[tm["Btm"]])
                            conv_pending.append(fin)
            while conv_pending:
                conv_pending.pop(0)()
            V(lambda e: e.tensor_tensor(out=xdt.rearrange("p a (h q) -> p a h q", q=64), in0=xs_tm.rearrange("p a (h q) -> p a h q", q=64),
                                        in1=dts.unsqueeze(3).broadcast_to([128, 4, 8, 64]), op=ALU.mult),
              reads=[tm["xs"], tm["dt"]], writes=[tm["xdt"]])

            sv = next_w(); P.dma("sp", sv.clk, [(sv.ap[:], wbmi[l, 5])], reads=list(tkw), writes=[sv.tok])
            if b == 0:
                load_rows(l, 1)
            for i in range(4):
                tt = 4 * b + i
                bk = 4 + i % 2
                P.group("pe", [lambda e, kc=kc, i=i, bk=bk: e.matmul(out=banks[bk][:], lhsT=uT[:, kc, i * 128:(i + 1) * 128], rhs=sv.ap[:, kc, :],
                                                                      start=(kc == 0), stop=(kc == 7)) for kc in range(8)],
                        reads=[sv.tok, t_uT[ub]], writes=[btok[bk]], same_ok=True)
                V(lambda e, tt=tt, bk=bk: e.tensor_copy(out=Vc[:, tt, :, 0:128], in_=banks[bk][:].rearrange("p (h c) -> p h c", c=128)),
                  reads=[btok[bk]], writes=[tm["V"]])

            sq = next_w(); P.dma("sp", sq.clk, [(sq.ap[:], wbmi[l, 3])], reads=list(tkw), writes=[sq.tok])
            sk = next_w(); P.dma("sp", sk.clk, [(sk.ap[:], wbmi[l, 4])], reads=list(tkw), writes=[sk.tok])
            def qk_proj(i):
                tt = 4 * b + i
                for wi, (which, sw) in enumerate((("q", sq), ("k", sk))):
                    bk = 4 + wi
                    P.group("pe", [lambda e, kc=kc, i=i, bk=bk, sw=sw: e.matmul(out=banks[bk][:], lhsT=uT[:, kc, i * 128:(i + 1) * 128], rhs=sw.ap[:, kc, :],
                                                                             start=(kc == 0), stop=(kc == 7)) for kc in range(8)],
                            reads=[sw.tok, t_uT[ub]], writes=[btok[bk]], same_ok=True)
                for wi in range(2):
                    bk = 4 + wi
                    qk = qk_tm[wi]
                    tq = tm[f"qk{wi}"]
                    A(lambda e, qk=qk, bk=bk: e.activation(out=qk, in_=banks[bk][:], func=AF.Copy), reads=[btok[bk]], writes=[tq])
                for wi in range(2):
                    qk = qk_tm[wi]
                    tq = tm[f"qk{wi}"]
                    qv = qk.rearrange("p (g d) -> p g d", d=64)
                    t1 = qv[:, :, 0:8]
                    t2 = qv[:, :, 8:16]
                    cb = cosT[:, tt, :].unsqueeze(1).broadcast_to([128, 8, 8])
                    sbb = sinT[:, tt, :].unsqueeze(1).broadcast_to([128, 8, 8])
                    rd = [tq, t_rope]
                    V(lambda e, t1=t1, cb=cb: e.tensor_tensor(out=ropet[:, :, 0, :], in0=t1, in1=cb, op=ALU.mult), reads=rd, writes=[tm["rope"]])
                    V(lambda e, t2=t2, sbb=sbb: e.tensor_tensor(out=ropet[:, :, 1, :], in0=t2, in1=sbb, op=ALU.mult), reads=rd, writes=[tm["rope"]])
                    V(lambda e, t2=t2, cb=cb: e.tensor_tensor(out=ropet[:, :, 2, :], in0=t2, in1=cb, op=ALU.mult), reads=rd, writes=[tm["rope"]])
                    V(lambda e, t1=t1, sbb=sbb: e.tensor_tensor(out=ropet[:, :, 3, :], in0=t1, in1=sbb, op=ALU.mult), reads=rd, writes=[tm["rope"]])
                    V(lambda e, t1=t1: e.tensor_tensor(out=t1, in0=ropet[:, :, 0, :], in1=ropet[:, :, 1, :], op=ALU.subtract), reads=[tm["rope"]], writes=[tq])
                    V(lambda e, t2=t2: e.tensor_tensor(out=t2, in0=ropet[:, :, 2, :], in1=ropet[:, :, 3, :], op=ALU.add), reads=[tm["rope"]], writes=[tq])

            def qk_T(i):
                tt = 4 * b + i
                QTi = QT2[i % 2]
                tQT = tm[f"QT{i % 2}"]
                for wi, which in enumerate(("q", "k")):
                    bk = 4 + wi
                    qk = qk_tm[wi]
                    tq = tm[f"qk{wi}"]
                    P.group("pe", [lambda e, h=h, qk=qk, bk=bk: e.transpose(out=banks[bk][:, h * 128:(h + 1) * 128], in_=qk[:, h * 128:(h + 1) * 128], identity=identf[:])
                                   for h in range(4)], reads=[tq, t_const], writes=[btok[bk]], same_ok=True)
                    if which == "q":
                        A(lambda e, bk=bk, QTi=QTi: e.activation(out=QTi, in_=banks[bk][:].rearrange("p (h c) -> p h c", c=128), func=AF.Copy, scale=0.125),
                          reads=[btok[bk]], writes=[tQT])
                    else:
                        A(lambda e, tt=tt, bk=bk: e.activation(out=KT[:, :, tt * 128:(tt + 1) * 128], in_=banks[bk][:].rearrange("p (h c) -> p h c", c=128), func=AF.Copy),
                          reads=[btok[bk]], writes=[tm["KT"]])

            def qk_prep(i):
                qk_proj(i)
                qk_T(i)

            def attn(i):
                qt = 4 * b + i
                QTi = QT2[i % 2]
                tQT = tm[f"QT{i % 2}"]
                nk = qt + 1
                groups = []
                for h in range(4):
                    ngrp = (nk + 3) // 4
                    for gi in range(ngrp):
                        groups.append((h, list(range(gi * 4, min(nk, gi * 4 + 4))), gi == 0, gi == ngrp - 1))

                def att_s1(n):
                    h, kts, first, last = groups[n]
                    sA = (0, 6, 4)[n % 3]
                    pi = n % 3
                    pt = PTr[pi]
                    tpt = tm[f"pt{pi}"]
                    nn = len(kts)
                    fns = []
                    for n_, kt in enumerate(kts):
                        for j in range(2):
                            ps = slice(64 * j, 64 * j + 64)
                            fns.append(lambda e, kt=kt, n_=n_, h=h, ps=ps, bk=sA + j: e.matmul(
                                out=banks[bk][:, n_ * 128:(n_ + 1) * 128], lhsT=KT[ps, h, kt * 128:(kt + 1) * 128], rhs=QTi[ps, h, :], start=True, stop=True))
                    P.group("pe", fns, reads=[tm["KT"], tQT], writes=[btok[sA], btok[sA + 1]], same_ok=True)
                    for j in range(2):
                        A(lambda e, pt=pt, bk=sA + j, nn=nn, j=j: e.activation(out=pt[:, j, 0:nn, :], in_=banks[bk][:, 0:nn * 128].rearrange("p (a c) -> p a c", c=128), func=AF.Exp),
                          reads=[btok[sA + j]], writes=[tpt])
                    if kts[-1] == qt:
                        V(lambda e, pt=pt, nn=nn: e.memset(pt[64:128, :, nn - 1, 0:64], 0.0), reads=[], writes=[tpt])

                def att_s2(n):
                    h, kts, first, last = groups[n]
                    pi = n % 3
                    pt = PTr[pi]
                    tpt = tm[f"pt{pi}"]
                    a0 = 2 + (h % 2)
                    fns = []
                    for j in range(2):
                        for n_, kt in enumerate(kts):
                            fns.append(lambda e, kt=kt, n_=n_, h=h, pt=pt, a0=a0, nk=nk, j=j, st=(first and j == 0 and n_ == 0): e.matmul(
                                out=banks[a0][:, j * 256:j * 256 + 129], lhsT=pt[:, j, n_, :], rhs=Vc[:, kt, h, 0:129], start=st, stop=(kt == nk - 1),
                                skip_group_check=True))
                    P.group("pe", fns, reads=[tpt, tm["V"]], writes=[btok[a0]], same_ok=True)
                    if last:
                        V(lambda e, a0=a0: e.reciprocal(out=rr[:, 4:5], in_=banks[a0][:, 128:129]), reads=[btok[a0]], writes=[tm["ss"]])
                        V(lambda e, a0=a0: e.reciprocal(out=rr[:, 5:6], in_=banks[a0][:, 384:385]), reads=[btok[a0]], writes=[tm["ss"]])
                        V(lambda e: e.tensor_tensor(out=rr[:, 5:6], in0=rr[:, 5:6], in1=nlam[:, l:l + 1], op=ALU.mult), reads=[tm["ss"], t_const], writes=[tm["ss"]])
                        V(lambda e, h=h, a0=a0: e.tensor_scalar(out=otmp[:, h * 128:(h + 1) * 128], in0=banks[a0][:, 0:128], scalar1=rr[:, 4:5], scalar2=None, op0=ALU.mult),
                          reads=[btok[a0], tm["ss"]], writes=[tm["otmp"]])
                        V(lambda e, h=h, a0=a0: e.scalar_tensor_tensor(out=otmp[:, h * 128:(h + 1) * 128], in0=banks[a0][:, 256:384], scalar=rr[:, 5:6],
                                                                       in1=otmp[:, h * 128:(h + 1) * 128], op0=ALU.mult, op1=ALU.add),
                          reads=[btok[a0], tm["ss"], tm["otmp"]], writes=[tm["otmp"]])

                for n in range(len(groups) + ATT_LA):
                    if n < len(groups):
                        att_s1(n)
                    if n - ATT_LA >= 0:
                        att_s2(n - ATT_LA)

            def o_norm(i):
                for h in range(4):
                    A(lambda e, h=h: e.activation(out=zs[:, h * 128:(h + 1) * 128], in_=otmp[:, h * 128:(h + 1) * 128], func=AF.Square, accum_out=ss[:, 4 + h:5 + h]),
                      reads=[tm["otmp"]], writes=[tm["zs"], tm["ss"]])
                A(lambda e: e.activation(out=rr[:, 0:4], in_=ss[:, 4:8], func=AF.Ln, scale=1.0 / 128, bias=epsT[:, 0:1]), reads=[tm["ss"], t_const], writes=[tm["ss"]])
                A(lambda e: e.activation(out=rr[:, 0:4], in_=rr[:, 0:4], func=AF.Exp, scale=-0.5), reads=[tm["ss"]], writes=[tm["ss"]])
                for h in range(4):
                    A(lambda e, h=h: e.activation(out=otmp[:, h * 128:(h + 1) * 128], in_=otmp[:, h * 128:(h + 1) * 128], func=AF.Copy, scale=rr[:, h:h + 1]),
                      reads=[tm["ss"], tm["otmp"]], writes=[tm["otmp"]])

            def o_T(i):
                P.group("pe", [lambda e, c=c: e.transpose(out=banks[5][:, c * 128:(c + 1) * 128], in_=otmp[:, c * 128:(c + 1) * 128], identity=identf[:]) for c in range(4)],
                        reads=[tm["otmp"], t_const], writes=[btok[5]], same_ok=True)
                A(lambda e, i=i: e.activation(out=yoT[:, 4:8, i * 128:(i + 1) * 128], in_=banks[5][:].rearrange("p (h c) -> p h c", c=128), func=AF.Copy, scale=sublnS[:, l:l + 1]),
                  reads=[btok[5], t_const], writes=[tm["yoT"]])

            qk_prep(0)
            qk_prep(1)
            for i in range(4):
                attn(i)
                if i + 2 < 4:
                    qk_proj(i + 2)
                o_norm(i)
                if i + 2 < 4:
                    qk_T(i + 2)
                o_T(i)

            sz = next_w(); P.dma("sp", sz.clk, [(sz.ap[:], wbmi[l, 0])], reads=list(tkw), writes=[sz.tok])
            def ssd_a0(i):
                for hg in range(2):
                    G(lambda e, i=i, hg=hg: e.tensor_tensor(out=aU2[hg], in0=Umat[:].unsqueeze(1).broadcast_to([128, 4, 128]),
                                                            in1=a_t[:, i, 4 * hg:4 * hg + 4].unsqueeze(2).broadcast_to([128, 4, 128]), op=ALU.mult),
                      reads=[tm["dt"], t_const], writes=[tm[f"aU{hg}"]])

            def ssd_a1(i):
                tsl = slice(i * 128, (i + 1) * 128)
                P.group("pe", [lambda e, g=g, tsl=tsl: e.matmul(out=banks[2][:, g * 128:(g + 1) * 128], lhsT=BCT[:, g, tsl], rhs=BCT[:, 2 + g, tsl], start=True, stop=True)
                               for g in range(2)], reads=[tm["BCT"]], writes=[btok[2]], same_ok=True)
                for hg in range(2):
                    P.group("pe", [lambda e, hg=hg: e.matmul(out=banks[hg][:], lhsT=onesf[:], rhs=aU2[hg].rearrange("p a b -> p (a b)"), start=True, stop=False),
                                   lambda e, hg=hg: e.matmul(out=banks[hg][:], lhsT=identf[:], rhs=negm[:], start=False, stop=True)],
                            reads=[tm[f"aU{hg}"], t_const], writes=[btok[hg]], same_ok=True)
                for hg in range(2):
                    for hh in range(4):
                        h = 4 * hg + hh
                        A(lambda e, h=h, hh=hh, hg=hg, i=i: e.activation(out=Eb2[hg][:, hh, :], in_=banks[hg][:, hh * 128:(hh + 1) * 128], func=AF.Exp,
                                                                       bias=nacs[:, i, h:h + 1]), reads=[btok[hg], tm["dt"]], writes=[tm[f"E{hg}"]])

            def ssd_a2(i):
                for hg in range(2):
                    V(lambda e, hg=hg: e.tensor_tensor(out=Mb[:, 4 * hg:4 * hg + 4, :], in0=banks[2][:, hg * 128:(hg + 1) * 128].unsqueeze(1).broadcast_to([128, 4, 128]),
                                                       in1=Eb2[hg], op=ALU.mult), reads=[btok[2], tm[f"E{hg}"]], writes=[tm["M"]])

            def ssd_b1(i):
                tsl = slice(i * 128, (i + 1) * 128)
                P.group("pe", [lambda e, h=h, i=i: e.matmul(out=banks[3][:, h * 64:(h + 1) * 64], lhsT=Mb[:, h, :], rhs=xdt[:, i, h * 64:(h + 1) * 64], start=True, stop=True)
                               for h in range(8)], reads=[tm["M"], tm["xdt"]], writes=[btok[3]], same_ok=True)
                P.group("pe", [lambda e, g=g, tsl=tsl: e.matmul(out=banks[4][:, g * 256:(g + 1) * 256], lhsT=BCT[:, 2 + g, tsl], rhs=prevb[:, g * 256:(g + 1) * 256], start=True, stop=True)
                               for g in range(2)], reads=[tm["BCT"], tm["prevb"]], writes=[btok[4]], same_ok=True)

            def ssd_b2a(i):
                G(lambda e, i=i: e.tensor_tensor(out=hv(xdt2), in0=hv(xs_tm[:, i, :]), in1=dtd[:, i, :].unsqueeze(2).broadcast_to([128, 8, 64]), op=ALU.mult),
                  reads=[tm["xs"], tm["dt"]], writes=[tm["xdt2"]])
                P.group("pe", [lambda e, g=g, i=i: e.matmul(out=banks[5][:, g * 256:(g + 1) * 256], lhsT=Btm[:, i, g * 128:(g + 1) * 128], rhs=xdt2[:, g * 256:(g + 1) * 256], start=True, stop=True)
                               for g in range(2)], reads=[tm["Btm"], tm["xdt2"]], writes=[btok[5]], same_ok=True)
                P.group("pe", [lambda e, kc=kc, i=i: e.matmul(out=banks[6][:], lhsT=uT[:, kc, i * 128:(i + 1) * 128], rhs=sz.ap[:, kc, :], start=(kc == 0), stop=(kc == 7))
                               for kc in range(8)], reads=[sz.tok, t_uT[ub]], writes=[btok[6]], same_ok=True)
                G(lambda e, i=i: e.tensor_tensor(out=hv(prev), in0=hv(prev), in1=cd[:, i, :].unsqueeze(2).broadcast_to([128, 8, 64]), op=ALU.mult),
                  reads=[tm["dt"], tm["prev"]], writes=[tm["prev"]])
                V(lambda e: e.tensor_tensor(out=prev, in0=banks[5][:], in1=prev, op=ALU.add), reads=[btok[5], tm["prev"]], writes=[tm["prev"]])
                A(lambda e: e.activation(out=prevb, in_=prev, func=AF.Copy), reads=[tm["prev"]], writes=[tm["prevb"]])
                A(lambda e: e.activation(out=zs, in_=banks[6][:], func=AF.Silu), reads=[btok[6]], writes=[tm["zs"]])

            def ssd_b2b(i):
                V(lambda e, i=i: e.tensor_tensor(out=hv(ytmp), in0=hv(banks[4][:]), in1=ea[:, i, :].unsqueeze(2).broadcast_to([128, 8, 64]), op=ALU.mult),
                  reads=[btok[4], tm["dt"]], writes=[tm["ytmp"]])
                V(lambda e: e.tensor_tensor(out=ytmp, in0=banks[3][:], in1=ytmp, op=ALU.add), reads=[btok[3], tm["ytmp"]], writes=[tm["ytmp"]])
                G(lambda e, i=i: e.tensor_tensor(out=hv(otmp), in0=hv(xs_tm[:, i, :]), in1=dsk[:, l, :].unsqueeze(2).broadcast_to([128, 8, 64]), op=ALU.mult),
                  reads=[tm["xs"], t_const], writes=[tm["otmp"]])
                V(lambda e: e.tensor_tensor(out=ytmp, in0=ytmp, in1=otmp, op=ALU.add), reads=[tm["otmp"], tm["ytmp"]], writes=[tm["ytmp"]])
                V(lambda e: e.tensor_tensor(out=ytmp, in0=ytmp, in1=zs, op=ALU.mult), reads=[tm["zs"], tm["ytmp"]], writes=[tm["ytmp"]])
                for g in range(2):
                    A(lambda e, g=g: e.activation(out=zs[:, g * 256:(g + 1) * 256], in_=ytmp[:, g * 256:(g + 1) * 256], func=AF.Square, accum_out=ss[:, g:g + 1]),
                      reads=[tm["ytmp"]], writes=[tm["zs"], tm["ss"]])
                A(lambda e: e.activation(out=rr[:, 0:2], in_=ss[:, 0:2], func=AF.Ln, scale=1.0 / 256, bias=epsT[:, 0:1]), reads=[tm["ss"], t_const], writes=[tm["ss"]])
                A(lambda e: e.activation(out=rr[:, 0:2], in_=rr[:, 0:2], func=AF.Exp, scale=-0.5), reads=[tm["ss"]], writes=[tm["ss"]])
                for g in range(2):
                    A(lambda e, g=g: e.activation(out=ytmp[:, g * 256:(g + 1) * 256], in_=ytmp[:, g * 256:(g + 1) * 256], func=AF.Copy, scale=rr[:, g:g + 1]),
                      reads=[tm["ss"], tm["ytmp"]], writes=[tm["ytmp"]])

            def ssd_yT(i):
                tsl = slice(i * 128, (i + 1) * 128)
                P.group("pe", [lambda e, c=c: e.transpose(out=banks[7][:, c * 128:(c + 1) * 128], in_=ytmp[:, c * 128:(c + 1) * 128], identity=identf[:]) for c in range(4)],
                        reads=[tm["ytmp"], t_const], writes=[btok[7]], same_ok=True)
                for c in range(4):
                    A(lambda e, c=c, tsl=tsl: e.activation(out=yoT[:, c, tsl], in_=banks[7][:, c * 128:(c + 1) * 128], func=AF.Copy, scale=ssdnw[:, l, c:c + 1]),
                      reads=[btok[7], t_const], writes=[tm["yoT"]])

            pa_next = None
            if b + 1 < NB:
                pa_next = phase_a_parts(l, 1, s, b + 1, ub, [(xhat_m, [tm["ytmp"], tm["otmp"]])])
                pa_next[0]()
            ssd_a0(0)
            ssd_a1(0)
            ssd_a2(0)
            ssd_a0(1)
            for i in range(4):
                if i + 1 < 4:
                    ssd_a1(i + 1)
                ssd_b1(i)
                if i + 1 < 4:
                    ssd_a2(i + 1)
                if i + 2 < 4:
                    ssd_a0(i + 2)
                ssd_b2a(i)
                if i >= 1:
                    ssd_yT(i - 1)
                ssd_b2b(i)
            ssd_yT(3)

            so = [None, None]
            for dmh in range(2):
                so[dmh] = next_w()
                P.dma("sp", so[dmh].clk, [(so[dmh].ap[:], wbmo[l, dmh])], reads=list(tkw), writes=[so[dmh].tok])
            if pa_next is not None:
                pa_next[1][0]()
            for i in range(4):
                for dmh in range(2):
                    bk = 4 + (i * 2 + dmh) % 4
                    P.group("pe", [lambda e, kc=kc, i=i, dmh=dmh, bk=bk: e.matmul(out=banks[bk][:], lhsT=yoT[:, kc, i * 128:(i + 1) * 128], rhs=so[dmh].ap[:, kc, :],
                                                                               start=(kc == 0), stop=(kc == 7)) for kc in range(8)],
                            reads=[so[dmh].tok, tm["yoT"]], writes=[btok[bk]], same_ok=True)
                    if dmh == 0 and pa_next is not None:
                        pa_next[2][i]()
                        if i + 1 < 4:
                            pa_next[1][i + 1]()
                    epilogue(4 * b + i, bk, dmh, dmh == 1)

    done = False
    try:
      for s in range(1 if dbg else 0):
        for b in range(NB):
            P.dma("sp", c_x[b], [(X[:, 4 * b + i, :], x_d[s, (4 * b + i) * 128:(4 * b + i + 1) * 128, :]) for i in range(4)],
                  writes=[Xtok[4 * b + i] for i in range(4)])
        if dbg:
            for l in range(nlayers):
                for sub in range(3):
                    P.barrier()
                    if sub == 1:
                        mixer_sublayer(l, s)
                    else:
                        ffn_sublayer(l, 0 if sub == 0 else 1, s)
    except StopBuild:
        P.emit()
        return nc, P
    if dbg:
        raise RuntimeError("dbg point not reached")
    P2 = None
    for s in range(nseq):
        for b in range(NB):
            P.dma("sp", c_x[b], [(X[:, 4 * b + i, :], x_d[s, (4 * b + i) * 128:(4 * b + i + 1) * 128, :]) for i in range(4)],
                  writes=[Xtok[4 * b + i] for i in range(4)])
        for l in range(nlayers):
            for sub in range(3):
                if sub != 0 or (l == 0 and s == 0):
                    P.barrier()
                if sub == 1:
                    mixer_sublayer(l, s)
                else:
                    ffn_sublayer(l, 0 if sub == 0 else 1, s)
                if stop is not None and (l, sub) == tuple(stop):
                    done = True
                    break
            if done:
                break
        for tt in range(NT):
            P.dma("sp", c_out, [(out_d[s, tt * 128:(tt + 1) * 128, :], X[:, tt, :])], reads=[Xtok[tt]], writes=[t_out])
        done = False
    P.finish([t_out])
    P.emit()
    return nc, P


_CACHE = {}


def _consts():
    identf = np.eye(128, dtype=np.float32)
    U = np.triu(np.ones((128, 128), dtype=np.float32))
    neg = np.where(np.arange(128)[None, :] < np.arange(128)[:, None], np.float32(NEG), np.float32(0.0)).astype(np.float32)
    negmask4 = np.tile(neg, (1, 4)).astype(np.float32)
    inv = np.array([500000.0 ** (-(i * 2.0) / 16.0) for i in range(8)], dtype=np.float32)
    invf = np.broadcast_to(inv[None, :], (128, 8)).copy()
    return identf, U, negmask4, invf


def _rep(a):
    return np.ascontiguousarray(np.broadcast_to(a[None], (128,) + a.shape))


def make_in_maps(inputs, ncores=8, nseq=2):
    f = lambda k: np.ascontiguousarray(np.asarray(inputs[k]))
    identf, U, negmask4, invf = _consts()
    shared = {
        "w_ada": f("w_ada"), "w_ffn_in": f("w_ffn_in"), "w_ffn_out": f("w_ffn_out"), "w_in": f("w_in"), "w_out": f("w_out"),
        "b_adaT": np.ascontiguousarray(f("b_ada").reshape(DEPTH, 72, 128).transpose(2, 0, 1)),
        "conv_wT": np.ascontiguousarray(f("conv_w").reshape(DEPTH, 4, 8, 128).transpose(3, 0, 1, 2)),
        "conv_bT": np.ascontiguousarray(f("conv_b").reshape(DEPTH, 8, 128).transpose(2, 0, 1)),
        "dtb": _rep(f("dt_bias")), "alog": _rep(f("a_log")), "dsk": _rep(f("d_skip")),
        "ssdnwT": np.ascontiguousarray(f("ssd_norm_w").reshape(DEPTH, 4, 128).transpose(2, 0, 1)),
        "sublnT": np.ascontiguousarray(f("subln_w").transpose(1, 0)),
        "dlam": _rep(f("diff_lambda")),
        "ln_g_rep": _rep(f("ln_g").reshape(-1)), "ln_b_rep": _rep(f("ln_b").reshape(-1)),
        "identf": identf, "Umat": U, "negmask4": negmask4, "invf": invf,
    }
    x = f("x"); c = f("c"); pos = f("positions")
    maps = []
    for ci in range(ncores):
        bs = [2 * ci + k for k in range(2)]
        m = dict(shared)
        m["x"] = np.ascontiguousarray(x[bs])
        m["cT"] = np.ascontiguousarray(c[bs].reshape(2, 8, 128).transpose(2, 1, 0))
        m["pos"] = np.ascontiguousarray(pos[bs].reshape(2, NT, 128).transpose(2, 0, 1)).astype(np.int32)
        maps.append(m)
    return maps


def kernel(**inputs):
    if "nc" not in _CACHE:
        _CACHE["nc"] = build()[0]
    nc = _CACHE["nc"]
    maps = make_in_maps(inputs)
    res = run_bass_kernel_spmd(nc, maps, core_ids=list(range(8)))
    out = np.concatenate([np.asarray(r["out"]) for r in res.results], axis=0)
    return out.astype(np.float32)
```
